# Optimizing a Trainium2 kernel written in Bass

```python
import math
import jax, jax.numpy as jnp
from jax import lax
import numpy as np

D_MODEL = 1024
BATCH = 8
SEQ = 2048
DEPTH = 1
DEC_BATCH = 128
DEC_SEQ = 4
PAST_LEN = 16384
PAGE_SIZE = 128

D_MIX = D_MODEL
GLA_HEADS = 4
GLA_DV = (D_MIX // 2) // GLA_HEADS
GLA_DK = GLA_DV // 2
GLA_KW = GLA_HEADS * GLA_DK
GLA_VW = GLA_HEADS * GLA_DV
GLA_GATE_RANK = 16
GLA_GATE_TAU = 16.0
SSD_INNER = D_MIX - GLA_VW
SSD_HEAD_DIM = 64
SSD_HEADS = SSD_INNER // SSD_HEAD_DIM
SSD_GROUPS = 2
SSD_STATE = 128
SSD_CONV = 4
SSD_CONV_CH = SSD_INNER + 2 * SSD_GROUPS * SSD_STATE
CHUNK = 64
MEM_LEN = 256
XA_HEADS = 4
XA_HEAD_DIM = D_MODEL // XA_HEADS
D_FF = -(-8 * D_MODEL // (3 * 256)) * 256
IN_SIZES = (GLA_KW, GLA_KW, GLA_VW, GLA_VW, GLA_GATE_RANK, SSD_INNER, SSD_CONV_CH, SSD_HEADS)
IN_COLS = sum(IN_SIZES)
EPS = 1e-6

kernel_name = 'hymba_gla_ssd_xmem_step'


def rmsnorm(x, w):
    xf = x.astype(jnp.float32)
    y = xf * lax.rsqrt(jnp.mean(xf * xf, axis=-1, keepdims=True) + EPS) * w.astype(jnp.float32)
    return y.astype(x.dtype)


def to_chunks(a, c):
    b, t = a.shape[:2]
    return jnp.swapaxes(a.reshape(b, t // c, c, *a.shape[2:]), 0, 1)


def from_chunks(a):
    nc, b, c = a.shape[:3]
    return jnp.swapaxes(a, 0, 1).reshape(b, nc * c, *a.shape[3:])


def gla_chunked(q, k, v, logf, s0):
    c = math.gcd(q.shape[1], CHUNK)
    causal = jnp.tril(jnp.ones((c, c), dtype=bool))

    def step(s, inp):
        qc, kc, vc, gc = inp
        b = jnp.cumsum(gc, axis=1)
        qt = qc * jnp.exp(b)
        kt = kc * jnp.exp(-b)
        att = jnp.where(causal, jnp.einsum('bthk,bshk->bhts', qt, kt), 0.0)
        o = jnp.einsum('bhts,bshv->bthv', att, vc) + jnp.einsum('bthk,bhkv->bthv', qt, s)
        bl = b[:, -1]
        s = jnp.exp(bl)[..., None] * s + jnp.einsum('bshk,bshv->bhkv', kc * jnp.exp(bl[:, None] - b), vc)
        return s, o

    s, o = lax.scan(step, s0, (to_chunks(q, c), to_chunks(k, c), to_chunks(v, c), to_chunks(logf, c)))
    return from_chunks(o), s


def ssd_chunked(x, dt, a_head, bh, ch, s0):
    c = math.gcd(x.shape[1], CHUNK)
    causal = jnp.tril(jnp.ones((c, c), dtype=bool))[None, :, :, None]

    def step(s, inp):
        xc, dtc, bc, cc = inp
        lc = jnp.cumsum(dtc * a_head, axis=1)
        seg = lc[:, :, None, :] - lc[:, None, :, :]
        decay = jnp.exp(jnp.where(causal, seg, -jnp.inf))
        cb = jnp.einsum('bthn,bshn->btsh', cc, bc)
        y = (jnp.einsum('btsh,bshp->bthp', cb * decay * dtc[:, None], xc)
             + jnp.einsum('bthn,bhpn->bthp', cc, s) * jnp.exp(lc)[..., None])
        ll = lc[:, -1]
        w = jnp.exp(ll[:, None] - lc) * dtc
        s = jnp.exp(ll)[:, :, None, None] * s + jnp.einsum('bsh,bshp,bshn->bhpn', w, xc, bc)
        return s, y

    s, y = lax.scan(step, s0, (to_chunks(x, c), to_chunks(dt, c), to_chunks(bh, c), to_chunks(ch, c)))
    return from_chunks(y), s


def mixer(hn, s_gla, s_ssm, conv_buf, lw):
    f32 = jnp.float32
    bsz, t, _ = hn.shape
    cuts = np.cumsum(IN_SIZES)[:-1].tolist()
    q, k, v, g, glr, z, xbc, dt_raw = jnp.split(hn @ lw['w_in'], cuts, axis=-1)
    q = q.reshape(bsz, t, GLA_HEADS, GLA_DK).astype(f32) * (GLA_DK ** -0.5)
    k = k.reshape(bsz, t, GLA_HEADS, GLA_DK).astype(f32)
    v = v.reshape(bsz, t, GLA_HEADS, GLA_DV).astype(f32)
    logf = jax.nn.log_sigmoid((glr @ lw['gla_gate_w2'] + lw['gla_gate_b']).astype(f32)) / GLA_GATE_TAU
    logf = logf.reshape(bsz, t, GLA_HEADS, GLA_DK)
    o, s_gla = gla_chunked(q, k, v, logf, s_gla.astype(f32))
    o = rmsnorm(o, lw['gla_norm_w']) * jax.nn.silu(g.reshape(bsz, t, GLA_HEADS, GLA_DV).astype(f32))
    o_gla = o.reshape(bsz, t, GLA_VW)
    full = jnp.concatenate([conv_buf.astype(xbc.dtype), xbc], axis=1)
    conv = lw['ssd_conv_b']
    for j in range(SSD_CONV):
        conv = conv + full[:, j:j + t] * lw['ssd_conv_w'][j]
    new_buf = full[:, full.shape[1] - (SSD_CONV - 1):]
    xbc_act = jax.nn.silu(conv.astype(f32))
    xs, bm, cm = jnp.split(xbc_act, [SSD_INNER, SSD_INNER + SSD_GROUPS * SSD_STATE], axis=-1)
    xs = xs.reshape(bsz, t, SSD_HEADS, SSD_HEAD_DIM)
    rep = SSD_HEADS // SSD_GROUPS
    bh = jnp.repeat(bm.reshape(bsz, t, SSD_GROUPS, SSD_STATE), rep, axis=2)
    chh = jnp.repeat(cm.reshape(bsz, t, SSD_GROUPS, SSD_STATE), rep, axis=2)
    dt = jax.nn.softplus((dt_raw + lw['ssd_dt_bias']).astype(f32))
    a_head = -jnp.exp(lw['ssd_A_log'].astype(f32))
    y, s_ssm = ssd_chunked(xs, dt, a_head, bh, chh, s_ssm.astype(f32))
    y = y + lw['ssd_D'].astype(f32)[:, None] * xs
    y = y.reshape(bsz, t, SSD_INNER) * jax.nn.silu(z.astype(f32))
    gs = SSD_INNER // SSD_GROUPS
    y = rmsnorm(y.reshape(bsz, t, SSD_GROUPS, gs), lw['ssd_norm_w'].reshape(SSD_GROUPS, gs)).reshape(bsz, t, SSD_INNER)
    mixed = jnp.concatenate([o_gla, y], axis=-1).astype(hn.dtype) @ lw['w_out']
    return mixed, s_gla, s_ssm, new_buf


def mem_kv(mem, mem_norm_w, w_xk, w_xv):
    b, m, _ = mem.shape
    mn = rmsnorm(mem, mem_norm_w)
    mk = (mn @ w_xk).reshape(b, m, XA_HEADS, XA_HEAD_DIM)
    mv = (mn @ w_xv).reshape(b, m, XA_HEADS, XA_HEAD_DIM)
    return mk, mv


def cross_attend(hn, mk, mv, w_xq, w_xo):
    b, t, _ = hn.shape
    q = (hn @ w_xq).reshape(b, t, XA_HEADS, XA_HEAD_DIM).astype(jnp.float32)
    s = jnp.einsum('bthd,bmhd->bhtm', q, mk.astype(jnp.float32)) * (XA_HEAD_DIM ** -0.5)
    p = jax.nn.softmax(s, axis=-1)
    o = jnp.einsum('bhtm,bmhd->bthd', p, mv.astype(jnp.float32)).reshape(b, t, D_MODEL)
    return o.astype(hn.dtype) @ w_xo


def decoder_layer(x, s_gla, s_ssm, conv_buf, mk, mv, lw):
    m, s_gla, s_ssm, conv_buf = mixer(rmsnorm(x, lw['ln_mix_pre']), s_gla, s_ssm, conv_buf, lw)
    h = x + rmsnorm(m, lw['ln_mix_post'])
    a = cross_attend(rmsnorm(h, lw['ln_xa_pre']), mk, mv, lw['w_xq'], lw['w_xo'])
    h = h + rmsnorm(a, lw['ln_xa_post'])
    hf = rmsnorm(h, lw['ln_ffn_pre'])
    f = (jax.nn.silu(hf @ lw['w_gate']) * (hf @ lw['w_up'])) @ lw['w_down']
    h = h + rmsnorm(f, lw['ln_ffn_post'])
    return h, s_gla, s_ssm, conv_buf


def setup_inputs(seed: int = 0) -> dict:
    key = jax.random.key(seed)
    ks = iter(jax.random.split(key, 48))

    def nrm(shape, scale):
        return jax.random.normal(next(ks), shape, jnp.float32) * scale

    def gain(n):
        return 1.0 + nrm((DEPTH, n), 0.02)

    L = DEPTH
    dt0 = jnp.exp(jax.random.uniform(next(ks), (L, SSD_HEADS), jnp.float32,
                                     math.log(1e-3), math.log(1e-1)))
    dt_bias = dt0 + jnp.log(-jnp.expm1(-dt0))
    a_log = jnp.log(jax.random.uniform(next(ks), (L, SSD_HEADS), jnp.float32, 1.0, 16.0))
    return {
        'x_prompt': nrm((BATCH, SEQ, D_MODEL), 1.0),
        'x_sample': nrm((DEC_BATCH, DEC_SEQ, D_MODEL), 1.0),
        'mem_prompt': nrm((BATCH, MEM_LEN, D_MODEL), 1.0),
        'state_gla': nrm((L, DEC_BATCH, GLA_HEADS, GLA_DK, GLA_DV), 0.1),
        'state_ssm': nrm((L, DEC_BATCH, SSD_HEADS, SSD_HEAD_DIM, SSD_STATE), 0.1),
        'state_conv': nrm((L, DEC_BATCH, SSD_CONV - 1, SSD_CONV_CH), 1.0),
        'cache_mem_k': nrm((L, DEC_BATCH, MEM_LEN, XA_HEADS, XA_HEAD_DIM), 1.0),
        'cache_mem_v': nrm((L, DEC_BATCH, MEM_LEN, XA_HEADS, XA_HEAD_DIM), 1.0),
        'ln_mix_pre': gain(D_MODEL),
        'ln_mix_post': gain(D_MODEL),
        'w_in': nrm((L, D_MODEL, IN_COLS), D_MODEL ** -0.5),
        'gla_gate_w2': nrm((L, GLA_GATE_RANK, GLA_KW), GLA_GATE_RANK ** -0.5),
        'gla_gate_b': nrm((L, GLA_KW), 0.1),
        'gla_norm_w': gain(GLA_DV),
        'ssd_conv_w': nrm((L, SSD_CONV, SSD_CONV_CH), SSD_CONV ** -0.5),
        'ssd_conv_b': nrm((L, SSD_CONV_CH), 0.02),
        'ssd_dt_bias': dt_bias,
        'ssd_A_log': a_log,
        'ssd_D': 1.0 + nrm((L, SSD_HEADS), 0.1),
        'ssd_norm_w': gain(SSD_INNER),
        'w_out': nrm((L, D_MIX, D_MODEL), D_MIX ** -0.5),
        'ln_xa_pre': gain(D_MODEL),
        'ln_xa_post': gain(D_MODEL),
        'mem_norm_w': gain(D_MODEL),
        'w_xq': nrm((L, D_MODEL, D_MODEL), D_MODEL ** -0.5),
        'w_xk': nrm((L, D_MODEL, D_MODEL), D_MODEL ** -0.5),
        'w_xv': nrm((L, D_MODEL, D_MODEL), D_MODEL ** -0.5),
        'w_xo': nrm((L, D_MODEL, D_MODEL), D_MODEL ** -0.5),
        'ln_ffn_pre': gain(D_MODEL),
        'ln_ffn_post': gain(D_MODEL),
        'w_gate': nrm((L, D_MODEL, D_FF), D_MODEL ** -0.5),
        'w_up': nrm((L, D_MODEL, D_FF), D_MODEL ** -0.5),
        'w_down': nrm((L, D_FF, D_MODEL), D_FF ** -0.5),
    }


def reference(x_prompt, x_sample, mem_prompt, state_gla, state_ssm, state_conv, cache_mem_k, cache_mem_v,
              ln_mix_pre, ln_mix_post, w_in, gla_gate_w2, gla_gate_b, gla_norm_w, ssd_conv_w, ssd_conv_b,
              ssd_dt_bias, ssd_A_log, ssd_D, ssd_norm_w, w_out, ln_xa_pre, ln_xa_post, mem_norm_w,
              w_xq, w_xk, w_xv, w_xo, ln_ffn_pre, ln_ffn_post, w_gate, w_up, w_down):
    bp = x_prompt.shape[0]
    hp, hs = x_prompt, x_sample
    gla_p, ssm_p, conv_p, mk_p, mv_p = [], [], [], [], []
    gla_s, ssm_s, conv_s = [], [], []
    for l in range(DEPTH):
        lw = dict(ln_mix_pre=ln_mix_pre[l], ln_mix_post=ln_mix_post[l], w_in=w_in[l],
                  gla_gate_w2=gla_gate_w2[l], gla_gate_b=gla_gate_b[l], gla_norm_w=gla_norm_w[l],
                  ssd_conv_w=ssd_conv_w[l], ssd_conv_b=ssd_conv_b[l], ssd_dt_bias=ssd_dt_bias[l],
                  ssd_A_log=ssd_A_log[l], ssd_D=ssd_D[l], ssd_norm_w=ssd_norm_w[l], w_out=w_out[l],
                  ln_xa_pre=ln_xa_pre[l], ln_xa_post=ln_xa_post[l], w_xq=w_xq[l], w_xo=w_xo[l],
                  ln_ffn_pre=ln_ffn_pre[l], ln_ffn_post=ln_ffn_post[l],
                  w_gate=w_gate[l], w_up=w_up[l], w_down=w_down[l])
        mk, mv = mem_kv(mem_prompt, mem_norm_w[l], w_xk[l], w_xv[l])
        s0_gla = jnp.zeros((bp, GLA_HEADS, GLA_DK, GLA_DV), jnp.float32)
        s0_ssm = jnp.zeros((bp, SSD_HEADS, SSD_HEAD_DIM, SSD_STATE), jnp.float32)
        c0 = jnp.zeros((bp, SSD_CONV - 1, SSD_CONV_CH), x_prompt.dtype)
        hp, sg, ss, cb = decoder_layer(hp, s0_gla, s0_ssm, c0, mk, mv, lw)
        gla_p.append(sg.astype(state_gla.dtype))
        ssm_p.append(ss.astype(state_ssm.dtype))
        conv_p.append(cb.astype(state_conv.dtype))
        mk_p.append(mk.astype(cache_mem_k.dtype))
        mv_p.append(mv.astype(cache_mem_v.dtype))
        hs, sg, ss, cb = decoder_layer(hs, state_gla[l], state_ssm[l], state_conv[l],
                                       cache_mem_k[l], cache_mem_v[l], lw)
        gla_s.append(sg.astype(state_gla.dtype))
        ssm_s.append(ss.astype(state_ssm.dtype))
        conv_s.append(cb.astype(state_conv.dtype))
    return (hp, hs, jnp.stack(gla_p), jnp.stack(ssm_p), jnp.stack(conv_p), jnp.stack(mk_p), jnp.stack(mv_p),
            jnp.stack(gla_s), jnp.stack(ssm_s), jnp.stack(conv_s))
```

```python
import numpy as np
from contextlib import ExitStack
import concourse.bass as bass
import concourse.mybir as mybir
from concourse.bass_utils import run_bass_kernel_spmd

F32 = mybir.dt.float32
BF16 = mybir.dt.bfloat16
ALU = mybir.AluOpType
AF = mybir.ActivationFunctionType
AX = mybir.AxisListType

D = 1024
DC = 8
NT = 16
SEQ = 2048
NSQ = 16
TS = 4
DFF = 2816
FC = 22
INC = 3096
EPS = 1e-6
SAMPLE_SSD_DISABLED = False
O_Q, O_K, O_V, O_G, O_GLR, O_Z, O_XBC, O_DT = 0, 256, 512, 1024, 1536, 1552, 2064, 3088

C_ID, C_MT, C_LS, C_ONE, C_MTS, C_LSS, C_SAME, C_SIND, C_NH, C_MROW = 0, 128, 256, 384, 512, 576, 640, 704, 720, 768
C_HM = 722
NCONST = 768 + 1024


class Buf:
    __slots__ = ("name", "w", "r", "psum")

    def __init__(self, name, psum=False):
        self.name = name
        self.w = None
        self.r = {}
        self.psum = psum


class KB:
    ENGS = ("pe", "act", "dve", "pool", "sp")

    def __init__(self, nc, n_dma_sems=32):
        self.nc = nc
        self.es = ExitStack()
        self.items = {e: [] for e in self.ENGS}
        self.seen = {e: {} for e in self.ENGS}
        self.n_dma_sems = n_dma_sems
        self.dma_cnt = [0] * n_dma_sems
        self.dma_rr = 0
        self.dma_rr_sw = 0
        self.targets = set()
        self.rec = None
        self.glue = False

    def sb(self, name, shape, dtype):
        return self.es.enter_context(self.nc.sbuf_tensor(name, list(shape), dtype))

    def ps(self, name, shape, dtype):
        return self.es.enter_context(self.nc.psum_tensor(name, list(shape), dtype))

    def _deps(self, eng, reads, writes):
        need = {}

        def add(dep, war=False):
            if dep is None:
                return
            s, i = dep
            if s == "pe" and eng == "pe":
                return
            if self.seen[eng].get(s, -1) >= i:
                return
            if need.get(s, -1) < i:
                need[s] = i

        for b in reads:
            add(b.w)
            if b.psum:
                for s, i in b.r.items():
                    add((s, i), war=True)
        for b in writes:
            add(b.w)
            for s, i in b.r.items():
                add((s, i), war=True)
        for s, i in need.items():
            self.seen[eng][s] = i
            if not s.startswith("dma"):
                self.targets.add((s, i))
        return need

    def capture(self, fn):
        old = self.rec
        self.rec = []
        fn()
        r = self.rec
        self.rec = old
        return r

    def replay(self, recs):
        for kind, eng, fn, R_, W_, _g, _c in recs:
            if kind == "op":
                self.op(eng, fn, R_, W_)
            else:
                self.dma(eng, fn, R_, W_)

    def op(self, eng, fn, reads=(), writes=(), cost=0.3):
        if self.rec is not None:
            self.rec.append(("op", eng, fn, tuple(reads), tuple(writes), self.glue, cost))
            return
        need = self._deps(eng, reads, writes)
        idx = len(self.items[eng])
        self.items[eng].append(dict(fn=fn, deps=need, dma=None))
        for b in reads:
            if b.r.get(eng, -1) < idx:
                b.r[eng] = idx
        for b in writes:
            b.w = (eng, idx)
            b.r = {}
        return idx

    def dma(self, eng, fn, reads=(), writes=(), cost=2.5):
        if self.rec is not None:
            self.rec.append(("dma", eng, fn, tuple(reads), tuple(writes), False, cost))
            return
        half = self.n_dma_sems // 2
        if eng == "pool":
            k = half + self.dma_rr_sw
            self.dma_rr_sw = (self.dma_rr_sw + 1) % (self.n_dma_sems - half)
        else:
            k = self.dma_rr
            self.dma_rr = (self.dma_rr + 1) % half
        sname = "dma%d" % k
        need = self._deps(eng, reads, writes)
        prev = self.dma_cnt[k]
        if prev > 0 and self.seen[eng].get(sname, -1) < prev:
            need[sname] = prev
            self.seen[eng][sname] = prev
        self.dma_cnt[k] += 1
        tick = self.dma_cnt[k]
        self.items[eng].append(dict(fn=fn, deps=need, dma=k))
        for b in reads:
            if b.r.get(sname, -1) < tick:
                b.r[sname] = tick
        for b in writes:
            b.w = (sname, tick)
            b.r = {}

    def barrier(self):
        last = {}
        for e in self.ENGS:
            last[e] = -1
            for i in range(len(self.items[e]) - 1, -1, -1):
                if self.items[e][i]["fn"] is not None and self.items[e][i]["dma"] is None:
                    last[e] = i
                    break
        for e in self.ENGS:
            need = {}
            for s, i in last.items():
                if s == e or i < 0:
                    continue
                if self.seen[e].get(s, -1) < i:
                    need[s] = i
                    self.seen[e][s] = i
                    self.targets.add((s, i))
            for k in range(self.n_dma_sems):
                sname = "dma%d" % k
                c = self.dma_cnt[k]
                if c > 0 and self.seen[e].get(sname, -1) < c:
                    need[sname] = c
                    self.seen[e][sname] = c
            self.items[e].append(dict(fn=None, deps=need, dma=None))

    def emit(self):
        nc = self.nc
        es = self.es
        sems = {e: es.enter_context(nc.semaphore("s_" + e)) for e in self.ENGS}
        dsems = [es.enter_context(nc.semaphore("s_dma%d" % k)) for k in range(self.n_dma_sems)]
        incval = {}
        for e in self.ENGS:
            c = 0
            for i, it in enumerate(self.items[e]):
                if (e, i) in self.targets:
                    assert it["fn"] is not None and it["dma"] is None, (e, i)
                    c += 1
                    incval[(e, i)] = c
        items = self.items
        dma_cnt = self.dma_cnt
        targets = self.targets
        n_dma_sems = self.n_dma_sems

        def run(e, engine):
            for i, it in enumerate(items[e]):
                for s, v in it["deps"].items():
                    if s.startswith("dma"):
                        engine.wait_ge(dsems[int(s[3:])], 16 * v)
                    else:
                        engine.wait_ge(sems[s], incval[(s, v)])
                if it["fn"] is None:
                    continue
                ins = it["fn"](engine)
                if it["dma"] is not None:
                    ins.then_inc(dsems[it["dma"]], 16)
                elif (e, i) in targets:
                    ins.then_inc(sems[e], 1)
            if e == "sp":
                for k in range(n_dma_sems):
                    if dma_cnt[k] > 0:
                        engine.wait_ge(dsems[k], 16 * dma_cnt[k])

        with nc.Block() as block:
            @block.tensor
            def _(eng):
                run("pe", eng)

            @block.scalar
            def _(eng):
                run("act", eng)

            @block.vector
            def _(eng):
                run("dve", eng)

            @block.gpsimd
            def _(eng):
                run("pool", eng)

            @block.sync
            def _(eng):
                run("sp", eng)
        es.close()


class Sched:
    LAT = 0.3

    def __init__(self):
        self.eng_free = {}
        self.bw = {}
        self.br = {}

    def _ready(self, rec):
        kind, eng, fn, R_, W_, glue, cost = rec
        t = self.eng_free.get(eng, 0.0)
        for b in R_:
            t = max(t, self.bw.get(id(b), 0.0) + self.LAT)
            if b.psum:
                t = max(t, self.br.get(id(b), 0.0) + self.LAT)
        for b in W_:
            t = max(t, self.bw.get(id(b), 0.0) + self.LAT, self.br.get(id(b), 0.0) + self.LAT)
        return t

    def _commit(self, rec):
        kind, eng, fn, R_, W_, glue, cost = rec
        st_ = self._ready(rec)
        if kind == "dma":
            self.eng_free[eng] = st_ + 0.6
            fin = st_ + cost
        else:
            fin = st_ + cost
            self.eng_free[eng] = fin
        for b in R_:
            if self.br.get(id(b), 0.0) < fin:
                self.br[id(b)] = fin
        for b in W_:
            self.bw[id(b)] = fin
            self.br[id(b)] = 0.0

    def merge(self, *lists):
        lists = [l for l in lists if l]
        pos = [0] * len(lists)
        rem = [sum(r[6] for r in l) for l in lists]
        out = []
        total = sum(len(l) for l in lists)
        while len(out) < total:
            best, bt = None, None
            for i, l in enumerate(lists):
                if pos[i] < len(l):
                    t = self._ready(l[pos[i]])
                    if bt is None or t < bt - 1e-9 or (abs(t - bt) <= 1e-9 and rem[i] > rem[best]):
                        best, bt = i, t
            while True:
                rec = lists[best][pos[best]]
                self._commit(rec)
                out.append(rec)
                rem[best] -= rec[6]
                pos[best] += 1
                if not (pos[best] < len(lists[best]) and lists[best][pos[best]][5]):
                    break
        return out


def interleave_prop(*lists, lead=None):
    if lead is None:
        lead = [0.0] * len(lists)
    lead = [b for l, b in zip(lists, lead) if l]
    lists = [l for l in lists if l]
    pos = [0] * len(lists)
    out = []
    total = sum(len(l) for l in lists)
    tot = [sum(r[6] for r in l) for l in lists]
    cum = [0.0] * len(lists)
    while len(out) < total:
        best, bf = None, None
        for i, l in enumerate(lists):
            if pos[i] < len(l):
                f = (cum[i] + 0.5 * l[pos[i]][6]) / tot[i] - lead[i]
                if bf is None or f < bf:
                    best, bf = i, f
        out.append(lists[best][pos[best]])
        cum[best] += lists[best][pos[best]][6]
        pos[best] += 1
        while pos[best] < len(lists[best]) and lists[best][pos[best]][5]:
            out.append(lists[best][pos[best]])
            cum[best] += lists[best][pos[best]][6]
            pos[best] += 1
    return out


def build_program(debug=False, stop=None):
    nc = bass.Bass("TRN2", target_bir_lowering=False)

    def din(name, shape):
        return nc.dram_tensor(name, list(shape), F32, kind="ExternalInput").ap()

    def dout(name, shape):
        return nc.dram_tensor(name, list(shape), F32, kind="ExternalOutput").ap()

    x_p = din("x_p", [SEQ, D])
    x_s = din("x_s", [64, D])
    mem = din("mem", [256, D])
    sgla = din("sgla", [NSQ, 4, 64, 128])
    sssm = din("sssm", [NSQ, 8, 64, 128])
    sconv = din("sconv", [NSQ * 3, D])
    ck = din("ck", [NSQ, 256, D])
    cv = din("cv", [NSQ, 256, D])
    consts = din("consts", [128, NCONST])
    ln_mix_pre = din("ln_mix_pre", [D]); ln_mix_post = din("ln_mix_post", [D])
    w_in = din("w_in", [D, INC]); gate_w2 = din("gla_gate_w2", [16, 256]); gate_b = din("gla_gate_b", [256])
    gla_nw = din("gla_norm_w", [128]); conv_w = din("ssd_conv_w", [4, D]); conv_b = din("ssd_conv_b", [D])
    dt_bias = din("ssd_dt_bias", [8]); a_log = din("ssd_A_log", [8]); ssd_D = din("ssd_D", [8])
    ssd_nw = din("ssd_norm_w", [512]); w_out = din("w_out", [D, D])
    ln_xa_pre = din("ln_xa_pre", [D]); ln_xa_post = din("ln_xa_post", [D]); mem_nw = din("mem_norm_w", [D])
    w_xq = din("w_xq", [D, D]); w_xk = din("w_xk", [D, D]); w_xv = din("w_xv", [D, D]); w_xo = din("w_xo", [D, D])
    ln_ffn_pre = din("ln_ffn_pre", [D]); ln_ffn_post = din("ln_ffn_post", [D])
    w_gate = din("w_gate", [D, DFF]); w_up = din("w_up", [D, DFF]); w_down = din("w_down", [DFF, D])

    y_p = dout("y_p", [SEQ, D]); y_s = dout("y_s", [64, D])
    gla_p = dout("gla_p", [4, 64, 128]); ssm_p = dout("ssm_p", [512, 128]); conv_p = dout("conv_p", [3, D])
    mk_p = dout("mk_p", [256, D]); mv_p = dout("mv_p", [256, D])
    gla_s = dout("gla_s", [NSQ, 4, 64, 128]); ssm_s = dout("ssm_s", [NSQ, 512, 128]); conv_s = dout("conv_s", [NSQ * 3, D])

    kb = KB(nc)
    sched = Sched()

    def interleave(*lists):
        return sched.merge(*lists)

    h_t = kb.sb("h", [128, NT + 1, D], F32)
    hB = [Buf("h%d" % i) for i in range(NT + 1)]
    identb_t = kb.sb("identb", [128, 128], BF16); B_identb = Buf("identb")
    identb = identb_t[:, :]
    cst = kb.sb("cst", [128, 768], F32); B_cst = Buf("cst")
    gainb = kb.sb("gainb", [128, D], F32); B_gainb = Buf("gainb")
    gcol = kb.sb("gcol", [128, 8], F32); B_gcol = Buf("gcol")
    RBYTES = 135600
    R = kb.sb("R", [128, RBYTES // 2], BF16)
    banks = [kb.ps("bank%d" % i, [128, 512], F32) for i in range(8)]
    bB = [Buf("bank%d" % i, psum=True) for i in range(8)]

    class Alloc:
        def __init__(self):
            self.off = 0

        def reset(self, off=0):
            self.off = off

        def __call__(self, name, shape, dtype, parts=128, at=None):
            n = 1
            for s in shape:
                n *= s
            size = n * (4 if dtype == F32 else 2)
            size = (size + 3) // 4 * 4
            if at is not None:
                off = at[0]
                assert size <= at[1], (name, size, at)
            else:
                off = self.off
                assert off + size <= RBYTES, (name, off, size)
                self.off += size
            self.last = (off, size)
            ap = R[:, off // 2:(off + size) // 2]
            if dtype == F32:
                ap = ap.bitcast(F32)
            ap = ap[:parts, :n]
            if len(shape) == 2:
                ap = ap.rearrange("p (a b) -> p a b", b=shape[1])
            elif len(shape) == 3:
                ap = ap.rearrange("p (a b c) -> p a b c", b=shape[1], c=shape[2])
            if at is not None:
                return ap, at[2]
            return ap, Buf(name)

        def slot(self, name, nbytes):
            off = self.off
            assert off + nbytes <= RBYTES, (name, off, nbytes)
            self.off += nbytes
            return [off, nbytes, Buf(name)]

    def sub(slot, o, n):
        return [slot[0] + o, n, slot[2]]

    alloc = Alloc()

    def bankv(i, shape, dtype=F32, parts=128, off=0):
        n = 1
        for s in shape:
            n *= s
        ap = banks[i][:, :]
        if dtype == BF16:
            ap = ap.bitcast(BF16)
        ap = ap[:parts, off:off + n]
        if len(shape) == 2:
            ap = ap.rearrange("p (a b) -> p a b", b=shape[1])
        elif len(shape) == 3:
            ap = ap.rearrange("p (a b c) -> p a b c", b=shape[1], c=shape[2])
        return ap

    def bc(ap, shape):
        return ap.unsqueeze(len(ap.shape)).to_broadcast(list(shape))

    def _n(ap):
        n = 1
        for d in ap.shape[1:]:
            n *= d
        return n

    def act(R_, W_, **kw):
        kb.op("act", lambda e: e.activation(**kw), R_, W_, cost=0.2 + _n(kw["out"]) / 1400.0)

    def tt(eng, R_, W_, **kw):
        c = (0.1 + _n(kw["out"]) / 900.0) if eng == "dve" else (0.25 + _n(kw["out"]) / 500.0)
        kb.op(eng, lambda e: e.tensor_tensor(**kw), R_, W_, cost=c)

    def ts(eng, R_, W_, **kw):
        c = (0.1 + _n(kw["out"]) / 900.0) if eng == "dve" else (0.25 + _n(kw["out"]) / 500.0)
        kb.op(eng, lambda e: e.tensor_scalar(**kw), R_, W_, cost=c)

    def stt(R_, W_, **kw):
        kb.op("dve", lambda e: e.scalar_tensor_tensor(**kw), R_, W_, cost=0.1 + _n(kw["out"]) / 900.0)

    def cp(eng, R_, W_, out, in_):
        if eng == "act":
            kb.op("act", lambda e: e.copy(out=out, in_=in_), R_, W_, cost=0.2 + _n(out) / 1400.0)
        else:
            kb.op(eng, lambda e: e.tensor_copy(out=out, in_=in_), R_, W_, cost=0.1 + _n(out) / 900.0)

    def mm(R_, W_, out, lhsT, rhs, start=True, stop=True, skip=False):
        n = _n(rhs) * (4 if lhsT.dtype == F32 else 1)
        c = 0.01 + n / 4800.0
        if skip:
            kb.op("pe", lambda e: e.matmul(out, lhsT=lhsT, rhs=rhs, start=start, stop=stop, skip_group_check=True), R_, W_, cost=c)
        else:
            kb.op("pe", lambda e: e.matmul(out, lhsT=lhsT, rhs=rhs, start=start, stop=stop), R_, W_, cost=c)

    def tr(R_, W_, out, in_, ident):
        kb.op("pe", lambda e: e.transpose(out=out, in_=in_, identity=ident), R_, W_, cost=0.035 * (4 if in_.dtype == F32 else 1))

    def ld(eng, W_, out, in_, R_=(), slow=False):
        if slow:
            kb.dma(eng, lambda e: e.dma_start(out=out, in_=in_, allow_slow_non_contiguous=True), R_, W_)
        else:
            kb.dma(eng, lambda e: e.dma_start(out=out, in_=in_), R_, W_)

    def st(R_, out, in_, eng="sp"):
        kb.dma(eng, lambda e: e.dma_start(out=out, in_=in_), R_, ())

    def rowb(vec, n):
        return vec.rearrange("(o n) -> o n", o=1).broadcast_to([128, n])

    def colv(vec, k):
        return vec.rearrange("(k p) -> p k", p=128)

    ld("sp", [B_cst], cst[:, :], consts[:, 0:768])
    cp("dve", [B_cst], [B_identb], identb, cst[:, C_ID:C_ID + 128])
    identf = cst[:, C_ID:C_ID + 128]
    neg_half = cst[:, C_NH:C_NH + 1]
    for ti in range(NT):
        ld("sp", [hB[ti]], h_t[:, ti, :], x_p[ti * 128:(ti + 1) * 128, :])
    ld("sp", [hB[NT]], h_t[:64, NT, :], x_s[:, :])

    def tile_T(ti):
        return 128 if ti < NT else 64

    def norm_T(src, Bsrc, T, dstT, BdstT, col0, scr, bank=2):
        junk, Bjunk, ss, Bss, xnb, Bxnb = scr
        act([Bsrc], [Bjunk, Bss], out=junk[:T, :], in_=src, func=AF.Square, accum_out=ss[:T, 0:1])
        ts("dve", [Bss], [Bss], out=ss[:T, 1:2], in0=ss[:T, 0:1], scalar1=1.0 / D, scalar2=EPS, op0=ALU.mult, op1=ALU.add)
        tt("pool", [Bss, B_cst], [Bss], out=ss[:T, 2:3], in0=ss[:T, 1:2], in1=neg_half[:T, :], op=ALU.pow)
        act([Bsrc, Bss], [Bxnb], out=xnb[:T, :], in_=src, func=AF.Copy, scale=ss[:T, 2:3])
        tp = bankv(bank, [8, 128], BF16)
        for c in range(DC):
            tr([Bxnb, B_identb], [bB[bank]], tp[:, c, :T], xnb[:T, c * 128:(c + 1) * 128], identb[:T, :T])
        tt("dve", [bB[bank], B_gcol], [BdstT], out=dstT[:, :, col0:col0 + T], in0=tp[:, :, :T],
           in1=bc(gcol[:, 0:8], [128, 8, T]), op=ALU.mult)

    def rstd_of(srcs, Bsrcs, T, ss, Bss, junk, Bjunk, ncol, width, scale):
        for j, s in enumerate(srcs):
            act(Bsrcs, [Bjunk, Bss], out=junk[:T, :width], in_=s, func=AF.Square, accum_out=ss[:T, j:j + 1])
        ts("dve", [Bss], [Bss], out=ss[:T, 4:4 + ncol], in0=ss[:T, 0:ncol], scalar1=scale, scalar2=EPS, op0=ALU.mult, op1=ALU.add)
        tt("pool", [Bss, B_cst], [Bss], out=ss[:T, 8:8 + ncol], in0=ss[:T, 4:4 + ncol],
           in1=neg_half[:T, :].to_broadcast([T, ncol]) if ncol > 1 else neg_half[:T, :], op=ALU.pow)

    alloc.reset()
    win, B_win = alloc("win", [8, INC], BF16)
    wout, B_wout = alloc("wout", [8, D], BF16)
    w2, B_w2 = alloc("w2", [256], BF16, parts=16)
    gateb_b, B_gateb = alloc("gateb_b", [256], F32)
    gnw_b, B_gnw = alloc("gnw_b", [128], F32)
    snw_b, B_snw = alloc("snw_b", [512], F32)
    small, B_small = alloc("small", [48], F32)
    cwt, B_cwt = alloc("cwt", [8, 5], F32)
    DmI, B_DmI = alloc("DmI", [8, 128], BF16)
    ssF, BssF = alloc("ssF", [16], F32)
    s_fx = alloc.slot("s_fx", 4096)
    xnb, Bxnb = alloc("xnb", [1024], BF16, at=sub(s_fx, 0, 2048))
    xnT, BxnT = alloc("xnT", [8, 128], BF16, at=sub(s_fx, 2048, 2048))
    c3o, Bc3o = alloc("c3o", [1024], F32, at=s_fx)
    ext, Bext = alloc("ext", [8, 131], F32)
    carry, Bcarry = alloc("carry", [8, 3], F32)
    s_cl = alloc.slot("s_cl", 4096)
    cacc, Bcacc = alloc("cacc", [8, 128], F32, at=s_cl)
    c3, Bc3 = alloc("c3", [8, 48], F32, at=s_cl)
    junkF, BjunkF = alloc("junkF", [1024], BF16, at=s_cl)
    xg, Bxg = alloc("xg", [256], F32, at=sub(s_cl, 0, 1024))
    spl, Bspl = alloc("spl", [256], F32, at=sub(s_cl, 1024, 1024))
    enbT, BenbT = alloc("enbT", [2, 128], F32, at=sub(s_cl, 2048, 1024))
    qtT, BqtT = alloc("qtT", [2, 128], BF16, at=sub(s_cl, 3072, 512))
    glrT, BglrT = alloc("glrT", [128], BF16, parts=16)
    scrF = (junkF, BjunkF, ssF, BssF, xnb, Bxnb)
    HO_SPEC = (("xbcT", [8, 128], BF16, 128), ("vb", [512], BF16, 128),
               ("gsw", [512], F32, 128), ("sz", [512], F32, 128), ("dtt", [80], F32, 128),
               ("ebT", [2, 128], F32, 128), ("ktT", [2, 128], BF16, 128), ("qtTz", [2, 2, 128], BF16, 128), ("ktok", [256], BF16, 128))
    HO = [dict((n, alloc(n + "0", sh, dt_, parts=pp)) for (n, sh, dt_, pp) in HO_SPEC)]
    s_p1 = alloc.slot("s_p1", 10624)
    ho1 = {}
    o1 = 0
    for (n, sh, dt_, pp) in HO_SPEC:
        ho1[n] = alloc(n + "1", sh, dt_, parts=pp, at=[s_p1[0] + o1, 1 << 20, Buf(n + "1")])
        o1 += alloc.last[1]
    HO.append(ho1)
    def _AA(name, shape, dt_, off, parts=128):
        return alloc(name, shape, dt_, parts=parts, at=[s_p1[0] + off, 1 << 20, Buf(name)])
    grot = []
    for r in range(3):
        o_ = r * 3072
        grot.append(dict(sgl=_AA("sgl%d" % r, [2, 128], F32, o_), sglb=_AA("sglb%d" % r, [2, 128], BF16, o_ + 1024),
                         qtTm=_AA("qtTm%d" % r, [2, 2, 64], BF16, o_ + 1536), vm=_AA("vm%d" % r, [512], BF16, o_ + 2048, parts=64)))
    srot = []
    for r in range(2):
        o_ = r * 4928
        srot.append(dict(sss=_AA("sss%d" % r, [4, 128], F32, o_), sssb=_AA("sssb%d" % r, [4, 128], BF16, o_ + 2048),
                         sssT=_AA("sssT%d" % r, [512], BF16, o_ + 3072), CTm=_AA("CTm%d" % r, [2, 64], BF16, o_ + 4096),
                         Bm=_AA("Bm%d" % r, [256], BF16, o_ + 4352, parts=64)))
    ss, Bss = alloc("ss", [16], F32)
    junk, Bjunk = alloc("junk", [512], BF16)
    t1, Bt1 = alloc("t1", [512], F32)
    yv, Byv = alloc("yv", [512], F32)
    s_L = alloc.slot("s_L", 4096)
    Lh, BLh = alloc("Lh", [8, 128], F32, at=s_L)
    t1x, Bt1x = alloc("t1x", [1024], F32, at=s_L)
    mix, Bmix = alloc("mix", [1024], BF16)
    mixT, BmixT = alloc("mixT", [8, 128], BF16)
    s_e = alloc.slot("s_e", 2048)
    Ee, BEe = alloc("Ee", [8, 128], BF16, at=s_e)
    s_m = alloc.slot("s_m", 2048)
    MT, BMT = alloc("MT", [8, 128], BF16, at=s_m)
    s_a = alloc.slot("s_a", 1024)
    attm, Battm = alloc("attm", [4, 128], BF16, at=s_a)
    xdt, Bxdt = alloc("xdt", [512], BF16, at=s_a)
    s_k = alloc.slot("s_k", 1024)
    xtok, Bxtok = alloc("xtok", [512], BF16, at=s_k)
    s_zx = alloc.slot("s_zx", 1024)
    xw, Bxw = alloc("xw", [512], BF16, at=s_zx)
    s_kc = alloc.slot("s_kc", 512)
    cbm, Bcbm = alloc("cbm", [2, 128], BF16, at=s_kc)
    Btok, BBtok = alloc("Btok", [256], BF16)
    s_Sg = alloc.slot("s_Sg", 2048)
    Sg, BSg = alloc("Sg", [2, 256], F32, at=s_Sg)
    dtaB, BdtaB = alloc("dtaB", [512], F32, at=s_Sg)
    Sgb, BSgb = alloc("Sgb", [2, 256], BF16)
    SsT, BSsT = alloc("SsT", [512], F32)
    SsTb, BSsTb = alloc("SsTb", [512], BF16)
    ellT, BellT = alloc("ellT", [4, 16], F32)

    WIN_GROUPS = [(0, INC)]
    B_winG = [Buf("win_g%d" % k) for k in range(len(WIN_GROUPS))]

    def winB(col):
        for k, (a_, b_) in enumerate(WIN_GROUPS):
            if a_ <= col < b_:
                return B_winG[k]
    for k, (a_, b_) in enumerate(WIN_GROUPS):
        for c in range(DC):
            ld("pool", [B_winG[k]], win[:, c, a_:b_], w_in[c * 128:(c + 1) * 128, a_:b_])
    ld("sp", [B_gcol], gcol[:, :], colv(ln_mix_pre, 8), slow=True)
    ld("sp", [B_gainb], gainb[:, :], rowb(ln_mix_post, D))
    ld("pool", [B_w2], w2[:, :], gate_w2[:, :])
    ld("sp", [B_gateb], gateb_b[:, :], rowb(gate_b, 256))
    ld("sp", [B_gnw], gnw_b[:, :], rowb(gla_nw, 128))
    ld("sp", [B_snw], snw_b[:, :], rowb(ssd_nw, 512))
    ld("sp", [B_small], small[:, 0:8], rowb(dt_bias, 8))
    ld("sp", [B_small], small[:, 8:16], rowb(a_log, 8))
    ld("sp", [B_small], small[:, 16:24], rowb(ssd_D, 8))
    for j in range(4):
        ld("sp", [B_cwt], cwt[:, :, j], colv(conv_w[j, :], 8), slow=True)
    ld("sp", [B_cwt], cwt[:, :, 4], colv(conv_b, 8), slow=True)
    for c in range(DC):
        ld("pool", [B_wout], wout[:, c, :], w_out[c * 128:(c + 1) * 128, :])
    act([B_small], [B_small], out=small[:, 24:32], in_=small[:, 8:16], func=AF.Exp)
    ts("dve", [B_small], [B_small], out=small[:, 24:32], in0=small[:, 24:32], scalar1=-1.0, scalar2=None, op0=ALU.mult)
    for hh in range(8):
        ts("dve", [B_cst, B_small], [B_DmI], out=DmI[:, hh, :], in0=identf, scalar1=small[:, 16 + hh:17 + hh], scalar2=None, op0=ALU.mult)
    kb.op("pool", lambda e: e.memset(Sg[:, :, :], 0.0), (), [BSg])
    kb.op("pool", lambda e: e.memset(Sgb[:, :, :], 0.0), (), [BSgb])
    kb.op("pool", lambda e: e.memset(SsT[:, :], 0.0), (), [BSsT])
    kb.op("pool", lambda e: e.memset(SsTb[:, :], 0.0), (), [BSsTb])
    kb.op("pool", lambda e: e.memset(carry[:, :, :], 0.0), (), [Bcarry])

    maskT_p = cst[:, C_MT:C_MT + 128]
    Ls_p = cst[:, C_LS:C_LS + 128]
    ones_p = cst[:, C_ONE:C_ONE + 128]
    maskT_s = cst[:64, C_MTS:C_MTS + 64]
    Ls_s = cst[:64, C_LSS:C_LSS + 64]
    same_s = cst[:64, C_SAME:C_SAME + 64]
    sind = cst[:64, C_SIND:C_SIND + 16]

    def a_front(ti, p):
        sample = ti == NT
        T = 64 if sample else 128
        nseq = NSQ if sample else 1
        Tq = TS if sample else 128
        H = HO[p]
        xbcT, BxbcT = H["xbcT"]; vb, Bvb = H["vb"]
        gsw, Bgsw = H["gsw"]; sz, Bsz = H["sz"]; dtt, Bdtt = H["dtt"]
        ebT, BebT = H["ebT"]; ktT, BktT = H["ktT"]; qtTz, BqtTz = H["qtTz"]; ktok, Bktok = H["ktok"]
        maskT = maskT_s if sample else maskT_p
        same = same_s if sample else ones_p
        extv = ext[:, :, :nseq * (3 + Tq)].rearrange("p c (b t) -> p c b t", t=3 + Tq)
        if sample:
            ld("sp", [Bc3o], c3o[:48, :], sconv[:, :])
            c3ps = bankv(0, [8, 48])
            for c in range(DC):
                tr([Bc3o, B_cst], [bB[0]], c3ps[:, c, :], c3o[:48, c * 128:(c + 1) * 128], identf[:48, :48])
            cp("dve", [bB[0]], [Bext], extv[:, :, :, 0:3], c3ps[:, :, :].rearrange("p c (b j) -> p c b j", j=3))
        norm_T(h_t[:T, ti, :], hB[ti], T, xnT, BxnT, 0, scrF, bank=0)

        def proj_fm(bank, nj, col0):
            ps_ = bankv(bank, [4, 128])
            for j in range(nj):
                for c in range(DC):
                    mm([winB(col0 + j * 128), BxnT], [bB[bank]], ps_[:, j, :T], win[:, c, col0 + j * 128:col0 + (j + 1) * 128], xnT[:, c, :T], c == 0, c == DC - 1)
            return ps_

        def proj_tm(bank, col0):
            ps_ = bankv(bank, [512], parts=T)
            for c in range(DC):
                mm([winB(col0), BxnT], [bB[bank]], ps_, xnT[:, c, :T], win[:, c, col0:col0 + 512], c == 0, c == DC - 1)
            return ps_

        extv = ext[:, :, :nseq * (3 + Tq)].rearrange("p c (b t) -> p c b t", t=3 + Tq)
        qk_ps = proj_fm(1, 4, O_Q)
        glr_ps = bankv(2, [128], parts=16)
        for c in range(DC):
            mm([winB(O_GLR), BxnT], [bB[2]], glr_ps[:, :T], win[:, c, O_GLR:O_GLR + 16], xnT[:, c, :T], c == 0, c == DC - 1)
        dt_ps = bankv(2, [8], parts=T, off=256)
        for c in range(DC):
            mm([winB(O_DT), BxnT], [bB[2]], dt_ps, xnT[:, c, :T], win[:, c, O_DT:O_DT + 8], c == 0, c == DC - 1)
        x0_ps = proj_fm(3, 4, O_XBC)
        cp("act", [bB[2]], [BglrT], glrT[:, :T], glr_ps[:, :T])
        tt("dve", [bB[2], B_small], [Bdtt], out=dtt[:T, 0:8], in0=dt_ps, in1=small[:T, 0:8], op=ALU.add)
        act([Bdtt], [Bdtt], out=dtt[:T, 8:16], in_=dtt[:T, 0:8], func=AF.Exp)
        act([Bdtt], [Bdtt], out=dtt[:T, 16:24], in_=dtt[:T, 8:16], func=AF.Ln, bias=1.0, scale=1.0)
        tt("dve", [Bdtt, B_small], [Bdtt], out=dtt[:T, 24:32], in0=dtt[:T, 16:24], in1=small[:T, 24:32], op=ALU.mult)
        lc_ps = bankv(2, [16], parts=T, off=320)
        mm([B_cst, Bdtt], [bB[2]], lc_ps[:, 0:8], maskT, dtt[:T, 24:32])
        mm([B_cst, Bdtt], [bB[2]], lc_ps[:, 8:16], same, dtt[:T, 24:32])
        cp("act", [bB[2]], [Bdtt], dtt[:T, 64:80], lc_ps[:, 0:16])
        act([Bdtt], [Bdtt], out=dtt[:T, 32:40], in_=dtt[:T, 64:72], func=AF.Exp)
        tt("dve", [Bdtt], [Bdtt], out=dtt[:T, 56:64], in0=dtt[:T, 72:80], in1=dtt[:T, 64:72], op=ALU.subtract)
        act([Bdtt], [Bdtt], out=dtt[:T, 40:48], in_=dtt[:T, 56:64], func=AF.Exp)
        act([Bdtt], [Bdtt], out=dtt[:T, 48:56], in_=dtt[:T, 72:80], func=AF.Exp)
        if not sample:
            cp("dve", [Bcarry], [Bext], extv[:, :, 0, 0:3], carry[:, :, :])
        cp("act", [bB[3]], [Bext], extv[:, 0:4, :, 3:3 + Tq], x0_ps[:, :, :T].rearrange("p c (b t) -> p c b t", t=Tq))
        lg_ps = bankv(0, [256], parts=T)
        mm([BglrT, B_w2], [bB[0]], lg_ps, glrT[:, :T], w2[:, :])
        tt("dve", [bB[0], B_gateb], [Bxg], out=xg[:T, :], in0=lg_ps, in1=gateb_b[:T, :], op=ALU.add)
        act([Bxg], [Bspl], out=spl[:T, :], in_=xg[:T, :], func=AF.Exp, scale=-1.0)
        act([Bspl], [Bspl], out=spl[:T, :], in_=spl[:T, :], func=AF.Ln, bias=1.0, scale=1.0)
        v_ps = proj_tm(2, O_V)
        cp("act", [bB[2]], [Bvb], vb[:T, :], v_ps)
        g_ps = proj_tm(3, O_G)
        bT_ps = bankv(0, [2, 128], off=256)
        for c in range(2):
            mm([Bspl, B_cst], [bB[0]], bT_ps[:, c, :T], spl[:T, c * 128:(c + 1) * 128], maskT)
        act([bB[0]], [BebT], out=ebT[:, :, :T], in_=bT_ps[:, :, :T], func=AF.Exp, scale=-1.0 / 16.0)
        act([bB[0]], [BenbT], out=enbT[:, :, :T], in_=bT_ps[:, :, :T], func=AF.Exp, scale=1.0 / 16.0)
        stt([bB[1], BebT], [BqtT], out=qtT[:, :, :T], in0=qk_ps[:, 0:2, :T], scalar=0.125, in1=ebT[:, :, :T], op0=ALU.mult, op1=ALU.mult)
        tt("dve", [bB[1], BenbT], [BktT], out=ktT[:, :, :T], in0=qk_ps[:, 2:4, :T], in1=enbT[:, :, :T], op=ALU.mult)
        for i in range(2):
            ts("dve", [BqtT, B_cst], [BqtTz], out=qtTz[:, :, i, :T], in0=qtT[:, :, :T], scalar1=cst[:, C_HM + i:C_HM + i + 1], scalar2=None, op0=ALU.mult)
        ktr_ps = bankv(0, [256], BF16, parts=T)
        for c in range(2):
            tr([BktT, B_identb], [bB[0]], ktr_ps[:, c * 128:(c + 1) * 128], ktT[:, c, :T], identb)
        cp("act", [bB[0]], [Bktok], ktok[:T, :], ktr_ps)
        x1_ps = proj_fm(1, 4, O_XBC + 512)
        cp("act", [bB[1]], [Bext], extv[:, 4:8, :, 3:3 + Tq], x1_ps[:, :, :T].rearrange("p c (b t) -> p c b t", t=Tq))
        z_ps = proj_tm(0, O_Z)
        if not sample:
            cp("dve", [Bext], [Bcarry], carry[:, :, :], extv[:, :, 0, Tq:Tq + 3])
        last_tile_of_seq = sample or ti == NT - 1
        if last_tile_of_seq:
            n3 = nseq * 3
            cp("dve", [Bext], [Bc3], c3[:, :, :n3].rearrange("p c (b j) -> p c b j", j=3), extv[:, :, :, Tq:Tq + 3])
            cops = [bankv(1, [512], parts=n3), bankv(2, [512], parts=n3)]
            cbk = [1, 2]
            for c in range(DC):
                tr([Bc3, B_cst], [bB[cbk[c // 4]]], cops[c // 4][:, (c % 4) * 128:(c % 4 + 1) * 128], c3[:, c, :n3], identf)
            for half in range(2):
                cp("dve", [bB[cbk[half]]], [Bc3o], c3o[:n3, half * 512:(half + 1) * 512], cops[half])
            st([Bc3o], (conv_s if sample else conv_p)[:, :], c3o[:n3, :])
        caccv = cacc[:, :, :T].rearrange("p c (b t) -> p c b t", t=Tq)
        for c in range(DC):
            act([Bext, B_cwt], [Bcacc], out=caccv[:, c], in_=extv[:, c, :, 3:3 + Tq], func=AF.Identity, scale=cwt[:, c, 3:4], bias=cwt[:, c, 4:5])
            for j in range(3):
                stt([Bext, B_cwt, Bcacc], [Bcacc], out=caccv[:, c], in0=extv[:, c, :, j:j + Tq], scalar=cwt[:, c, j:j + 1], in1=caccv[:, c],
                    op0=ALU.mult, op1=ALU.add)
        act([bB[3]], [Bgsw], out=gsw[:T, :], in_=g_ps, func=AF.Silu)
        kb.glue = True
        act([bB[0]], [Bsz], out=sz[:T, :], in_=z_ps, func=AF.Silu)
        act([Bcacc], [BxbcT], out=xbcT[:, :, :T], in_=cacc[:, :, :T], func=AF.Silu)
        kb.glue = False
        tt("pool", [Bgsw, B_gnw], [Bgsw], out=gsw[:T, :].rearrange("p (h v) -> p h v", v=128), in0=gsw[:T, :].rearrange("p (h v) -> p h v", v=128),
           in1=gnw_b[:T, :].unsqueeze(1).to_broadcast([T, 4, 128]), op=ALU.mult)

    def a_back(ti, p):
        sample = ti == NT
        T = 64 if sample else 128
        H = HO[p]
        xbcT, BxbcT = H["xbcT"]; vb, Bvb = H["vb"]
        gsw, Bgsw = H["gsw"]; sz, Bsz = H["sz"]; dtt, Bdtt = H["dtt"]
        ebT, BebT = H["ebT"]; ktT, BktT = H["ktT"]; qtTz, BqtTz = H["qtTz"]; ktok, Bktok = H["ktok"]
        maskT = maskT_s if sample else maskT_p
        Ls = Ls_s if sample else Ls_p
        same = same_s if sample else ones_p
        att_ps = bankv(6, [4, 128], parts=T)
        for hh in range(4):
            mm([BktT, BqtTz], [bB[6]], att_ps[:, hh, :T], ktT[:, hh // 2, :T], qtTz[:, hh // 2, hh % 2, :T])
        tt("dve", [bB[6], B_cst], [Battm], out=attm[:T, :, :T], in0=att_ps[:, :, :T],
           in1=maskT.unsqueeze(1).to_broadcast([T, 4, T]), op=ALU.mult)
        o_ps = bankv(7, [512], parts=T)
        for hh in range(4):
            c = hh // 2
            mm([Battm, Bvb], [bB[7]], o_ps[:, hh * 128:(hh + 1) * 128], attm[:T, hh, :T], vb[:T, hh * 128:(hh + 1) * 128],
               (hh == 0) if sample else True, False, skip=sample)
            if not sample:
                mm([BqtTz, BSgb], [bB[7]], o_ps[:, hh * 128:(hh + 1) * 128], qtTz[:, c, hh % 2, :T],
                   Sgb[:, c, (hh % 2) * 128:(hh % 2 + 1) * 128], False, True)
        if not sample:
            Pg_ps = bankv(4, [2, 256])
            for c in range(2):
                mm([Bktok, Bvb], [bB[4]], Pg_ps[:, c, :], ktok[:T, c * 128:(c + 1) * 128], vb[:T, c * 256:(c + 1) * 256])
            tt("dve", [bB[4], BSg], [BSg], out=Sg[:, :, :], in0=Pg_ps[:, :, :], in1=Sg[:, :, :], op=ALU.add)
            tt("dve", [BSg, BebT], [BSg], out=Sg[:, :, :], in0=Sg[:, :, :], in1=bc(ebT[:, :, T - 1], [128, 2, 256]), op=ALU.mult)
            cp("act", [BSg], [BSgb], Sgb[:, :, :], Sg[:, :, :])
            if ti == NT - 1:
                for hh in range(4):
                    c, r0 = hh // 2, (hh % 2) * 64
                    st([BSg], gla_p[hh, :, :], Sg[r0:r0 + 64, c, (hh % 2) * 128:(hh % 2 + 1) * 128])
        else:
            NR = 3
            for r in range(NR):
                kb.op("pool", lambda e, r=r: e.memset(grot[r]["qtTm"][0][:, :, :, :], 0.0), (), [grot[r]["qtTm"][1]])

            def gla_load(b):
                sgl, Bsgl = grot[b % NR]["sgl"]
                ld("sp", [Bsgl], sgl[:, :, :], sgla[b].rearrange("(c i) k v -> (i k) c v", i=2))
            for b in range(min(NR - 1, NSQ)):
                gla_load(b)
            for b in range(NSQ):
                if b + NR - 1 < NSQ:
                    gla_load(b + NR - 1)
                R_ = grot[b % NR]
                (sgl, Bsgl), (sglb, Bsglb), (qtTm, BqtTm), (vm, Bvm) = R_["sgl"], R_["sglb"], R_["qtTm"], R_["vm"]
                c0 = b * TS
                if b >= NR:
                    pc = (b - NR) * TS
                    kb.op("pool", lambda e, qtTm=qtTm, pc=pc: e.memset(qtTm[:, :, :, pc:pc + TS], 0.0), (), [BqtTm])
                kb.op("dve", lambda e, qtTm=qtTm, c0=c0: e.tensor_copy(out=qtTm[:, :, :, c0:c0 + TS], in_=qtTz[:, :, :, c0:c0 + TS]), [BqtTz], [BqtTm])
                ts("dve", [Bvb, B_cst], [Bvm], out=vm[:64, :], in0=vb[:64, :], scalar1=sind[:, b:b + 1], scalar2=None, op0=ALU.mult)
                cp("act", [Bsgl], [Bsglb], sglb[:, :, :], sgl[:, :, :])
                for hh in range(4):
                    c = hh // 2
                    mm([BqtTm, Bsglb], [bB[7]], o_ps[:, hh * 128:(hh + 1) * 128], qtTm[:, c, hh % 2, :],
                       sglb[:, c, :], False, b == NSQ - 1, skip=True)
                Pg_ps = bankv(4, [2, 256])
                for c in range(2):
                    mm([Bktok, Bvm], [bB[4]], Pg_ps[:, c, :], ktok[:64, c * 128:(c + 1) * 128], vm[:64, c * 256:(c + 1) * 256])
                for i in range(2):
                    tt("dve", [bB[4], Bsgl], [Bsgl], out=sgl[i * 64:(i + 1) * 64, :, :], in0=Pg_ps[i * 64:(i + 1) * 64, :, i * 128:(i + 1) * 128],
                       in1=sgl[i * 64:(i + 1) * 64, :, :], op=ALU.add)
                tt("dve", [Bsgl, BebT], [Bsgl], out=sgl[:, :, :], in0=sgl[:, :, :], in1=bc(ebT[:, :, b * TS + TS - 1], [128, 2, 128]), op=ALU.mult)
                st([Bsgl], gla_s[b].rearrange("(c i) k v -> (i k) c v", i=2), sgl[:, :, :], eng="pool")
        rstd_of([o_ps[:, hh * 128:(hh + 1) * 128] for hh in range(4)], [bB[7]], T, ss, Bss, junk, Bjunk, 4, 128, 1.0 / 128)
        tt("dve", [bB[7], Bss], [Bt1], out=t1[:T, :].rearrange("p (h v) -> p h v", v=128), in0=o_ps.rearrange("p (h v) -> p h v", v=128),
           in1=bc(ss[:T, 8:12], [T, 4, 128]), op=ALU.mult)
        tt("dve", [Bt1, Bgsw], [Bmix], out=mix[:T, 0:512], in0=t1[:T, :], in1=gsw[:T, :], op=ALU.mult)
        if sample:
            kb.barrier()
        BT_ = xbcT[:, 4:6, :]
        CT_ = xbcT[:, 6:8, :]
        tt("dve", [B_cst, Bdtt], [BLh], out=Lh[:T, :, :T], in0=Ls.unsqueeze(1).to_broadcast([T, 8, T]),
           in1=bc(dtt[:T, 24:32], [T, 8, T]), op=ALU.mult)
        segb = [6, 4]
        seg_ps = [bankv(6, [4, 128], parts=T), bankv(4, [4, 128], parts=T)]
        for hh in range(8):
            mm([BLh, B_cst], [bB[segb[hh // 4]]], seg_ps[hh // 4][:, hh % 4, :T], Lh[:T, hh, :T], maskT)
        for half in range(2):
            act([bB[segb[half]]], [BEe], out=Ee[:T, half * 4:(half + 1) * 4, :T], in_=seg_ps[half][:, :, :T], func=AF.Exp)
        cb_ps = bankv(5, [2, 128], parts=T, off=256)
        for g in range(2):
            mm([BxbcT], [bB[5]], cb_ps[:, g, :T], BT_[:, g, :T], CT_[:, g, :T])
        tt("dve", [bB[5], B_cst], [Bcbm], out=cbm[:T, :, :T], in0=cb_ps[:, :, :T], in1=maskT.unsqueeze(1).to_broadcast([T, 2, T]), op=ALU.mult)
        for g in range(2):
            tt("dve", [Bcbm, BEe], [BMT], out=MT[:T, g * 4:(g + 1) * 4, :T], in0=cbm[:T, g, :T].unsqueeze(1).to_broadcast([T, 4, T]),
               in1=Ee[:T, g * 4:(g + 1) * 4, :T], op=ALU.mult)
        xtr_ps = bankv(7, [768], BF16, parts=T)
        for c in range(6):
            tr([BxbcT, B_identb], [bB[7]], xtr_ps[:, c * 128:(c + 1) * 128], xbcT[:, c, :T], identb)
        tt("dve", [bB[7], Bdtt], [Bxdt], out=xdt[:T, :].rearrange("p (h q) -> p h q", q=64), in0=xtr_ps[:, 0:512].rearrange("p (h q) -> p h q", q=64),
           in1=bc(dtt[:T, 16:24], [T, 8, 64]), op=ALU.mult)
        cp("act", [bB[7]], [Bxtok], xtok[:T, :], xtr_ps[:, 0:512])
        cp("act", [bB[7]], [BBtok], Btok[:T, :], xtr_ps[:, 512:768])
        tt("dve", [Bxdt, Bdtt], [Bxw], out=xw[:T, :].rearrange("p (h q) -> p h q", q=64), in0=xdt[:T, :].rearrange("p (h q) -> p h q", q=64),
           in1=bc(dtt[:T, 40:48], [T, 8, 64]), op=ALU.mult)
        y_ps = bankv(6, [512], parts=T)
        for hh in range(8):
            mm([BMT, Bxdt], [bB[6]], y_ps[:, hh * 64:(hh + 1) * 64], MT[:T, hh, :T], xdt[:T, hh * 64:(hh + 1) * 64], True, False)
            mm([B_DmI, Bxtok], [bB[6]], y_ps[:, hh * 64:(hh + 1) * 64], DmI[:T, hh, :T], xtok[:T, hh * 64:(hh + 1) * 64], False, True)
        yi_ps = bankv(4, [512], parts=T)
        if not sample:
            for g in range(2):
                mm([BxbcT, BSsTb], [bB[4]], yi_ps[:, g * 256:(g + 1) * 256], CT_[:, g, :T], SsTb[:, g * 256:(g + 1) * 256])
            PT_ps = bankv(5, [512])
            for g in range(2):
                mm([BBtok, Bxw], [bB[5]], PT_ps[:, g * 256:(g + 1) * 256], Btok[:T, g * 128:(g + 1) * 128], xw[:T, g * 256:(g + 1) * 256])
        else:
            cp("dve", [Bdtt], [BdtaB], dtaB[:64, :].rearrange("p (h q) -> p h q", q=64), bc(dtt[:64, 24:32], [64, 8, 64]))
            ellT_ps = bankv(5, [4, 16])
            for c in range(4):
                mm([BdtaB, B_cst], [bB[5]], ellT_ps[:, c, :], dtaB[:64, c * 128:(c + 1) * 128], sind)
            act([bB[5]], [BellT], out=ellT[:, :, :], in_=ellT_ps[:, :, :], func=AF.Exp)
            for r in range(2):
                kb.op("pool", lambda e, r=r: e.memset(srot[r]["CTm"][0][:, :, :], 0.0), (), [srot[r]["CTm"][1]])
            def ssd_load(b):
                sss, Bsss = srot[b % 2]["sss"]
                ld("sp", [Bsss], sss[:, :, :], sssm[b].rearrange("h q n -> (h q) n").rearrange("(c p) n -> p c n", p=128))
            ssd_load(0)
            for b in range(NSQ):
                if b + 1 < NSQ:
                    ssd_load(b + 1)
                R_ = srot[b % 2]
                (sss, Bsss), (sssb, Bsssb), (sssT, BsssT), (CTm, BCTm), (Bm, BBm) = R_["sss"], R_["sssb"], R_["sssT"], R_["CTm"], R_["Bm"]
                c0 = b * TS
                if b >= 2:
                    pc = (b - 2) * TS
                    kb.op("pool", lambda e, CTm=CTm, pc=pc: e.memset(CTm[:, :, pc:pc + TS], 0.0), (), [BCTm])
                kb.op("dve", lambda e, CTm=CTm, c0=c0: e.tensor_copy(out=CTm[:, :, c0:c0 + TS], in_=CT_[:, :, c0:c0 + TS]), [BxbcT], [BCTm])
                ts("dve", [BBtok, B_cst], [BBm], out=Bm[:64, :], in0=Btok[:64, :], scalar1=sind[:, b:b + 1], scalar2=None, op0=ALU.mult)
                cp("act", [Bsss], [Bsssb], sssb[:, :, :], sss[:, :, :])
                sT_ps = bankv(5, [512], BF16)
                for c in range(4):
                    tr([Bsssb, B_identb], [bB[5]], sT_ps[:, c * 128:(c + 1) * 128], sssb[:, c, :], identb)
                cp("act", [bB[5]], [BsssT], sssT[:, :], sT_ps)
                for g in range(2):
                    mm([BCTm, BsssT], [bB[4]], yi_ps[:, g * 256:(g + 1) * 256], CTm[:, g, :], sssT[:, g * 256:(g + 1) * 256],
                       b == 0 and g == 0, b == NSQ - 1, skip=True)
                Ps_ps = bankv(7, [4, 128])
                for c in range(4):
                    mm([Bxw, BBm], [bB[7]], Ps_ps[:, c, :], xw[:64, c * 128:(c + 1) * 128], Bm[:64, (c // 2) * 128:(c // 2 + 1) * 128])
                tt("dve", [Bsss, BellT], [Bsss], out=sss[:, :, :], in0=sss[:, :, :], in1=bc(ellT[:, :, b], [128, 4, 128]), op=ALU.mult)
                tt("dve", [bB[7], Bsss], [Bsss], out=sss[:, :, :], in0=Ps_ps[:, :, :], in1=sss[:, :, :], op=ALU.add)
                st([Bsss], ssm_s[b].rearrange("(c p) n -> p c n", p=128), sss[:, :, :], eng="pool")
        tt("dve", [bB[4], Bdtt], [Bt1], out=t1[:T, :].rearrange("p (h q) -> p h q", q=64), in0=yi_ps.rearrange("p (h q) -> p h q", q=64),
           in1=bc(dtt[:T, 32:40], [T, 8, 64]), op=ALU.mult)
        tt("dve", [bB[6], Bt1], [Byv], out=yv[:T, :], in0=y_ps, in1=t1[:T, :], op=ALU.add)
        tt("dve", [Byv, Bsz], [Byv], out=yv[:T, :], in0=yv[:T, :], in1=sz[:T, :], op=ALU.mult)
        if not sample:
            tt("dve", [BSsT, Bdtt], [BSsT], out=SsT[:, :].rearrange("p (h q) -> p h q", q=64), in0=SsT[:, :].rearrange("p (h q) -> p h q", q=64),
               in1=bc(dtt[:, 48:56], [128, 8, 64]), op=ALU.mult)
            tt("dve", [bB[5], BSsT], [BSsT], out=SsT[:, :], in0=PT_ps, in1=SsT[:, :], op=ALU.add)
            cp("act", [BSsT], [BSsTb], SsTb[:, :], SsT[:, :])
            if ti == NT - 1:
                fin_ps = bankv(7, [4, 128])
                for c in range(4):
                    tr([BSsT, B_cst], [bB[7]], fin_ps[:, c, :], SsT[:, c * 128:(c + 1) * 128], identf)
                cp("dve", [bB[7]], [Bt1], t1[:, :].rearrange("p (c n) -> p c n", n=128), fin_ps[:, :, :])
                st([Bt1], ssm_p.rearrange("(c p) n -> p c n", p=128), t1[:, :].rearrange("p (c n) -> p c n", n=128))
        rstd_of([yv[:T, g * 256:(g + 1) * 256] for g in range(2)], [Byv], T, ss, Bss, junk, Bjunk, 2, 256, 1.0 / 256)
        for g in range(2):
            stt([Byv, Bss, B_snw], [Bmix], out=mix[:T, 512 + g * 256:512 + (g + 1) * 256], in0=yv[:T, g * 256:(g + 1) * 256],
                scalar=ss[:T, 8 + g:9 + g], in1=snw_b[:T, g * 256:(g + 1) * 256], op0=ALU.mult, op1=ALU.mult)
        mT_ps = bankv(7, [8, 128], BF16)
        for c in range(DC):
            tr([Bmix, B_identb], [bB[7]], mT_ps[:, c, :T], mix[:T, c * 128:(c + 1) * 128], identb[:T, :T])
        cp("act", [bB[7]], [BmixT], mixT[:, :, :T], mT_ps[:, :, :T])
        mob = [6, 4]
        mo_ps = [bankv(6, [512], parts=T), bankv(4, [512], parts=T)]
        for half in range(2):
            for c in range(DC):
                mm([BmixT, B_wout], [bB[mob[half]]], mo_ps[half], mixT[:, c, :T], wout[:, c, half * 512:(half + 1) * 512], c == 0, c == DC - 1)
        post_norm_residual(mo_ps, [bB[6], bB[4]], T, ti, ss, Bss, junk, Bjunk, t1x, Bt1x)

    def post_norm_residual(ps2, Bps2, T, ti, ss, Bss, junk, Bjunk, tmp, Btmp):
        for half in range(2):
            act([Bps2[half]], [Bjunk, Bss], out=junk[:T, :512], in_=ps2[half], func=AF.Square, accum_out=ss[:T, half:half + 1])
        tt("dve", [Bss], [Bss], out=ss[:T, 3:4], in0=ss[:T, 0:1], in1=ss[:T, 1:2], op=ALU.add)
        ts("dve", [Bss], [Bss], out=ss[:T, 4:5], in0=ss[:T, 3:4], scalar1=1.0 / D, scalar2=EPS, op0=ALU.mult, op1=ALU.add)
        tt("pool", [Bss, B_cst], [Bss], out=ss[:T, 8:9], in0=ss[:T, 4:5], in1=neg_half[:T, :], op=ALU.pow)
        if tmp.shape[-1] >= 1024:
            for half in range(2):
                stt([Bps2[half], Bss, B_gainb], [Btmp], out=tmp[:T, half * 512:(half + 1) * 512], in0=ps2[half], scalar=ss[:T, 8:9],
                    in1=gainb[:T, half * 512:(half + 1) * 512], op0=ALU.mult, op1=ALU.mult)
            tt("pool", [Btmp, hB[ti]], [hB[ti]], out=h_t[:T, ti, :], in0=h_t[:T, ti, :], in1=tmp[:T, :], op=ALU.add)
        else:
            for half in range(2):
                stt([Bps2[half], Bss, B_gainb], [Btmp], out=tmp[:T, :512], in0=ps2[half], scalar=ss[:T, 8:9],
                    in1=gainb[:T, half * 512:(half + 1) * 512], op0=ALU.mult, op1=ALU.mult)
                tt("pool", [Btmp, hB[ti]], [hB[ti]], out=h_t[:T, ti, half * 512:(half + 1) * 512], in0=h_t[:T, ti, half * 512:(half + 1) * 512],
                   in1=tmp[:T, :512], op=ALU.add)

    tilesA = list(range(NT + 1))
    if stop is not None and stop[0] == "A":
        tilesA = stop[1]
    if stop is not None and stop[0] == "X":
        tilesA = stop[3]
    ptA = [t for t in tilesA if t < NT]
    nA = len(ptA)
    for i in range(nA + 1):
        streams = []
        if i < nA:
            streams.append(kb.capture(lambda: a_front(ptA[i], i % 2)))
        if 0 <= i - 1 < nA:
            streams.append(kb.capture(lambda: a_back(ptA[i - 1], (i - 1) % 2)))
        kb.replay(interleave_prop(*streams))
    if NT in tilesA:
        kb.barrier()
        a_front(NT, 0)
        a_back(NT, 0)
    if stop is not None and stop[0] == "A":
        kb.emit()
        return nc

    if debug:
        dbg_h1 = dout("dbg_h1", [SEQ + 64, D])
        for ti in range(NT + 1):
            T = tile_T(ti)
            st([hB[ti]], dbg_h1[ti * 128:ti * 128 + T, :], h_t[:T, ti, :])

    kb.barrier()
    alloc.reset()
    wxq, B_wxq = alloc("wxq", [8, D], BF16)
    wxo, B_wxo = alloc("wxo", [8, D], BF16)
    s_wxk = alloc.slot("s_wxk", 16384)
    s_wxv = alloc.slot("s_wxv", 16384)
    wxk, B_wxk = alloc("wxk", [8, D], BF16, at=s_wxk)
    wxv, B_wxv = alloc("wxv", [8, D], BF16, at=s_wxv)
    junk, Bjunk = alloc("junkB", [1024], BF16)
    ss, Bss = alloc("ssB", [16], F32)
    xnb, Bxnb = alloc("xnbB", [1024], BF16)
    scr = (junk, Bjunk, ss, Bss, xnb, Bxnb)
    memt, Bmemt = alloc("memt", [2, D], F32)
    mnT, BmnT = alloc("mnT", [8, 256], BF16)
    kvo, Bkvo = alloc("kvo", [1024], F32)
    KT, BKT = alloc("KT", [8, 256], BF16)
    Vb, BVb = alloc("Vb", [2, D], BF16)
    hnT, BhnT = alloc("hnT", [8, 128], BF16)
    qTP = [alloc("qT%d" % r, [8, 128], BF16) for r in range(2)]
    sc, Bsc = alloc("sc", [4, 256], F32)
    pb, Bpb = alloc("pb", [4, 256], BF16)
    pTP = [alloc("pT%d" % r, [8, 128], BF16) for r in range(2)]
    ob, Bob = alloc("ob", [1024], BF16)
    oT, BoT = alloc("oT", [8, 128], BF16)
    t1x, Bt1x = alloc("t1xB", [1024], F32)
    smxP = [alloc("smx%d" % r, [16], F32) for r in range(2)]
    junk3, Bjunk3 = alloc("junk3", [512], BF16)
    ss3, Bss3 = alloc("ss3", [16], F32)
    def _BA(name, shape, dt_, off):
        return alloc(name, shape, dt_, at=[s_wxk[0] + off, 1 << 20, Buf(name)])
    kfR = [_BA("kf%d" % r, [2, D], F32, r * 8192) for r in range(3)]
    k16R = [_BA("k16%d" % r, [2, D], BF16, 24576 + r * 4096) for r in range(2)]
    kTbR = [alloc("kTb%d" % r, [8, 256], BF16) for r in range(2)]
    qTmR = [alloc("qTm%d" % r, [8, 64], BF16) for r in range(2)]
    pTmR = [alloc("pTm%d" % r, [8, 64], BF16) for r in range(2)]
    for c in range(DC):
        ld("pool", [B_wxk], wxk[:, c, :], w_xk[c * 128:(c + 1) * 128, :])
    for c in range(DC):
        ld("pool", [B_wxv], wxv[:, c, :], w_xv[c * 128:(c + 1) * 128, :])
    for c in range(DC):
        ld("pool", [B_wxq], wxq[:, c, :], w_xq[c * 128:(c + 1) * 128, :])
    for c in range(DC):
        ld("pool", [B_wxo], wxo[:, c, :], w_xo[c * 128:(c + 1) * 128, :])
    ld("sp", [B_gcol], gcol[:, :], colv(mem_nw, 8), slow=True)
    ld("sp", [Bmemt], memt[:, :, :], mem.rearrange("(c p) d -> p c d", p=128))
    for mc in range(2):
        norm_T(memt[:, mc, :], Bmemt, 128, mnT, BmnT, mc * 128, scr)
    for (wt, Bwt, outd, isk) in ((wxk, B_wxk, mk_p, True), (wxv, B_wxv, mv_p, False)):
        for mc in range(2):
            kv_ps = [bankv(0, [512]), bankv(1, [512])]
            for half in range(2):
                for c in range(DC):
                    mm([BmnT, Bwt], [bB[half]], kv_ps[half], mnT[:, c, mc * 128:(mc + 1) * 128], wt[:, c, half * 512:(half + 1) * 512], c == 0, c == DC - 1)
            for half in range(2):
                cp("act", [bB[half]], [Bkvo], kvo[:, half * 512:(half + 1) * 512], kv_ps[half])
            st([Bkvo], outd[mc * 128:(mc + 1) * 128, :], kvo[:, :])
            if not isk:
                cp("dve", [Bkvo], [BVb], Vb[:, mc, :], kvo[:, :])
        if isk:
            for j in range(8):
                kt_ps = bankv(3 + j % 2, [256])
                for c in range(DC):
                    mm([BmnT, Bwt], [bB[3 + j % 2]], kt_ps, wt[:, c, j * 128:(j + 1) * 128], mnT[:, c, :], c == 0, c == DC - 1)
                cp("act", [bB[3 + j % 2]], [BKT], KT[:, j, :], kt_ps)
    ld("sp", [B_gcol], gcol[:, :], colv(ln_xa_pre, 8), slow=True)
    ld("sp", [B_gainb], gainb[:, :], rowb(ln_xa_post, D))

    def b_s1(ti, p):
        T = tile_T(ti)
        qT, BqT = qTP[p]
        norm_T(h_t[:T, ti, :], hB[ti], T, hnT, BhnT, 0, scr, bank=1)
        for half in range(2):
            q_ps = bankv(0, [4, 128])
            for j in range(4):
                for c in range(DC):
                    mm([BhnT, B_wxq], [bB[0]], q_ps[:, j, :T], wxq[:, c, (half * 4 + j) * 128:(half * 4 + j + 1) * 128], hnT[:, c, :T], c == 0, c == DC - 1)
            cp("act", [bB[0]], [BqT], qT[:, half * 4:(half + 1) * 4, :T], q_ps[:, :, :T])

    def b_softmax(T, p):
        smx, Bsmx = smxP[p]
        pT, BpT = pTP[p]
        sc_ps = [bankv(3, [2, 256], parts=T), bankv(4, [2, 256], parts=T)]
        for half in range(2):
            cp("act", [bB[3 + half]], [Bsc], sc[:T, half * 2:(half + 1) * 2, :], sc_ps[half])
        kb.op("dve", lambda e: e.tensor_reduce(out=smx[:T, 0:4], in_=sc[:T, :, :], axis=AX.X, op=ALU.max), [Bsc], [Bsmx])
        ts("dve", [Bsmx], [Bsmx], out=smx[:T, 4:8], in0=smx[:T, 0:4], scalar1=-1.0 / 16.0, scalar2=None, op0=ALU.mult)
        for hh in range(4):
            act([Bsc, Bsmx], [Bpb, Bsmx], out=pb[:T, hh, :], in_=sc[:T, hh, :], func=AF.Exp, scale=1.0 / 16.0, bias=smx[:T, 4 + hh:5 + hh],
                accum_out=smx[:T, 8 + hh:9 + hh])
        kb.op("dve", lambda e: e.reciprocal(out=smx[:T, 12:16], in_=smx[:T, 8:12]), [Bsmx], [Bsmx])
        pT_ps = bankv(2, [8, 128], BF16)
        for hh in range(4):
            for mc in range(2):
                tr([Bpb, B_identb], [bB[2]], pT_ps[:, hh * 2 + mc, :T], pb[:T, hh, mc * 128:(mc + 1) * 128], identb[:T, :T])
        cp("act", [bB[2]], [BpT], pT[:, :, :T], pT_ps[:, :, :T])

    def b_s2(ti, p):
        T = tile_T(ti)
        qT, BqT = qTP[p]
        sc_ps = [bankv(3, [2, 256], parts=T), bankv(4, [2, 256], parts=T)]
        for hh in range(4):
            for dc in range(2):
                mm([BqT, BKT], [bB[3 + hh // 2]], sc_ps[hh // 2][:, hh % 2, :], qT[:, hh * 2 + dc, :T], KT[:, hh * 2 + dc, :], dc == 0, dc == 1)
        b_softmax(T, p)

    def b_tail(ti, p):
        T = tile_T(ti)
        smx, Bsmx = smxP[p]
        o_ps = [bankv(5, [512], parts=T), bankv(6, [512], parts=T)]
        for half in range(2):
            tt("dve", [bB[5 + half], Bsmx], [Bob], out=ob[:T, half * 512:(half + 1) * 512].rearrange("p (h d) -> p h d", d=256),
               in0=o_ps[half].rearrange("p (h d) -> p h d", d=256), in1=bc(smx[:T, 12 + half * 2:14 + half * 2], [T, 2, 256]), op=ALU.mult)
        oT_ps = bankv(7, [8, 128], BF16)
        for c in range(DC):
            tr([Bob, B_identb], [bB[7]], oT_ps[:, c, :T], ob[:T, c * 128:(c + 1) * 128], identb[:T, :T])
        cp("act", [bB[7]], [BoT], oT[:, :, :T], oT_ps[:, :, :T])
        a_ps = [bankv(5, [512], parts=T), bankv(6, [512], parts=T)]
        for half in range(2):
            for c in range(DC):
                mm([BoT, B_wxo], [bB[5 + half]], a_ps[half], oT[:, c, :T], wxo[:, c, half * 512:(half + 1) * 512], c == 0, c == DC - 1)
        post_norm_residual(a_ps, [bB[5], bB[6]], T, ti, ss3, Bss3, junk3, Bjunk3, t1x, Bt1x)

    def b_s3(ti, p):
        T = tile_T(ti)
        pT, BpT = pTP[p]
        o_ps = [bankv(5, [512], parts=T), bankv(6, [512], parts=T)]
        for hh in range(4):
            for mc in range(2):
                mm([BpT, BVb], [bB[5 + hh // 2]], o_ps[hh // 2][:, (hh % 2) * 256:(hh % 2 + 1) * 256], pT[:, hh * 2 + mc, :T],
                   Vb[:, mc, hh * 256:(hh + 1) * 256], mc == 0, mc == 1)
        b_tail(ti, p)

    def b_sample():
        ti, T, p = NT, 64, 0
        qT, BqT = qTP[p]
        pT, BpT = pTP[p]
        b_s1(ti, p)
        for r in range(2):
            kb.op("pool", lambda e, r=r: e.memset(qTmR[r][0][:, :, :], 0.0), (), [qTmR[r][1]])
            kb.op("pool", lambda e, r=r: e.memset(pTmR[r][0][:, :, :], 0.0), (), [pTmR[r][1]])
        sc_ps = [bankv(3, [2, 256], parts=T), bankv(4, [2, 256], parts=T)]
        o_ps = [bankv(5, [512], parts=T), bankv(6, [512], parts=T)]
        NKR = 3

        def kv_load(src, b):
            kf, Bkf = kfR[b % NKR]
            ld("sp", [Bkf], kf[:, :, :], src[b].rearrange("(c p) d -> p c d", p=128))

        def kv_cast(b):
            kf, Bkf = kfR[b % NKR]
            k16, Bk16 = k16R[b % 2]
            cp("act", [Bkf], [Bk16], k16[:, 0, :], kf[:, 0, :])
            cp("dve", [Bkf], [Bk16], k16[:, 1, :], kf[:, 1, :])
            return k16, Bk16

        for b in range(NKR - 1):
            kv_load(ck, b)
        for b in range(NSQ):
            if b + NKR - 1 < NSQ:
                kv_load(ck, b + NKR - 1)
            k16, Bk16 = kv_cast(b)
            kTb, BkTb = kTbR[b % 2]
            for mc in range(2):
                kt_ps = bankv(mc, [8, 128], BF16)
                for j in range(8):
                    tr([Bk16, B_identb], [bB[mc]], kt_ps[:, j, :], k16[:, mc, j * 128:(j + 1) * 128], identb)
                cp("act" if mc == 0 else "dve", [bB[mc]], [BkTb], kTb[:, :, mc * 128:(mc + 1) * 128], kt_ps[:, :, :])
            qTm, BqTm = qTmR[b % 2]
            if b >= 2:
                kb.op("pool", lambda e, qTm=qTm, pc=(b - 2) * TS: e.memset(qTm[:, :, pc:pc + TS], 0.0), (), [BqTm])
            kb.op("dve", lambda e, qTm=qTm, c0=b * TS: e.tensor_copy(out=qTm[:, :, c0:c0 + TS], in_=qT[:, :, c0:c0 + TS]), [BqT], [BqTm])
            for hh in range(4):
                for dc in range(2):
                    mm([BqTm, BkTb], [bB[3 + hh // 2]], sc_ps[hh // 2][:, hh % 2, :], qTm[:, hh * 2 + dc, :], kTb[:, hh * 2 + dc, :],
                       b == 0 and dc == 0 and hh % 2 == 0, b == NSQ - 1 and dc == 1, skip=True)
        for b in range(NKR - 1):
            kv_load(cv, b)
        b_softmax(T, p)
        for b in range(NSQ):
            if b + NKR - 1 < NSQ:
                kv_load(cv, b + NKR - 1)
            k16, Bk16 = kv_cast(b)
            pTm, BpTm = pTmR[b % 2]
            if b >= 2:
                kb.op("pool", lambda e, pTm=pTm, pc=(b - 2) * TS: e.memset(pTm[:, :, pc:pc + TS], 0.0), (), [BpTm])
            kb.op("dve", lambda e, pTm=pTm, c0=b * TS: e.tensor_copy(out=pTm[:, :, c0:c0 + TS], in_=pT[:, :, c0:c0 + TS]), [BpT], [BpTm])
            for hh in range(4):
                for mc in range(2):
                    mm([BpTm, Bk16], [bB[5 + hh // 2]], o_ps[hh // 2][:, (hh % 2) * 256:(hh % 2 + 1) * 256], pTm[:, hh * 2 + mc, :],
                       k16[:, mc, hh * 256:(hh + 1) * 256], b == 0 and mc == 0 and hh % 2 == 0, b == NSQ - 1 and mc == 1, skip=True)
        b_tail(ti, p)

    def phaseB_all(tiles):
        ptiles = [t for t in tiles if t < NT]
        n = len(ptiles)
        for i in range(n + 2):
            streams = []
            leads = []
            if i < n:
                streams.append(kb.capture(lambda: b_s1(ptiles[i], i % 2)))
                leads.append(0.0)
            if 0 <= i - 1 < n:
                streams.append(kb.capture(lambda: b_s2(ptiles[i - 1], (i - 1) % 2)))
                leads.append(0.05)
            if 0 <= i - 2 < n:
                streams.append(kb.capture(lambda: b_s3(ptiles[i - 2], (i - 2) % 2)))
                leads.append(0.10)
            kb.replay(interleave_prop(*streams, lead=leads))
        if NT in tiles:
            kb.barrier()
            b_sample()

    tilesB = list(range(NT + 1))
    if SAMPLE_SSD_DISABLED:
        tilesB = list(range(NT))
    if stop is not None and stop[0] == "B":
        tilesB = stop[1]
    if stop is not None and stop[0] == "X":
        tilesB = stop[4]
    phaseB_all(tilesB)
    if stop is not None and stop[0] == "B":
        kb.emit()
        return nc

    if debug:
        dbg_h2 = dout("dbg_h2", [SEQ + 64, D])
        for ti in range(NT + 1):
            T = tile_T(ti)
            st([hB[ti]], dbg_h2[ti * 128:ti * 128 + T, :], h_t[:T, ti, :])

    kb.barrier()
    alloc.reset()
    wg, B_wg = alloc("wg", [8, DFF], BF16)
    wd, B_wd = alloc("wd", [FC, D], BF16)
    NRING = 2
    ring = [alloc("wu%d" % r, [8, 256], BF16) for r in range(NRING)]
    s_jx = alloc.slot("s_jx", 2048)
    junk, Bjunk = alloc("junkC", [1024], BF16, at=s_jx)
    ss, Bss = alloc("ssC", [16], F32)
    ss3, Bss3 = alloc("ssC3", [16], F32)
    xnb, Bxnb = alloc("xnbC", [1024], BF16, at=s_jx)
    scr = (junk, Bjunk, ss, Bss, xnb, Bxnb)
    hfT, BhfT = alloc("hfT", [8, 512], BF16)
    actT, BactT = alloc("actT", [FC, 512], BF16)
    sgt = [alloc("sgt%d" % r, [512], F32) for r in range(2)]
    junk3 = sgt[0][0].bitcast(BF16)
    Bjunk3 = sgt[0][1]
    tmp3, Btmp3 = sgt[1]

    for c in range(DC):
        ld("pool", [B_wg], wg[:, c, :], w_gate[c * 128:(c + 1) * 128, :])
    ld("sp", [B_gcol], gcol[:, :], colv(ln_ffn_pre, 8), slow=True)
    ld("sp", [B_gainb], gainb[:, :], rowb(ln_ffn_post, D))

    groups = [(list(range(g * 4, g * 4 + 4))) for g in range(4)] + [[NT]]
    gcols = {}

    def c_norms(gi):
        col = 0
        cols = []
        for ti in groups[gi]:
            T = tile_T(ti)
            norm_T(h_t[:T, ti, :], hB[ti], T, hfT, BhfT, col, scr)
            cols.append(col)
            col += T
        gcols[gi] = cols

    def c_down(gi):
        cols = gcols[gi]
        for k, ti in enumerate(groups[gi]):
            T = tile_T(ti)
            f_ps = [bankv(5, [512], parts=T), bankv(6, [512], parts=T)]
            for half in range(2):
                for f in range(FC):
                    mm([BactT, B_wd], [bB[5 + half]], f_ps[half], actT[:, f, cols[k]:cols[k] + T], wd[:, f, half * 512:(half + 1) * 512], f == 0, f == FC - 1)
            post_norm_residual(f_ps, [bB[5], bB[6]], T, ti, ss3, Bss3, junk3, Bjunk3, tmp3, Btmp3)
            if ti < NT:
                st([hB[ti]], y_p[ti * 128:(ti + 1) * 128, :], h_t[:, ti, :])
            else:
                st([hB[ti]], y_s[:, :], h_t[:64, ti, :])

    c_norms(0)
    ring_i = 0
    for gi, grp in enumerate(groups):
        GT = sum(tile_T(ti) for ti in grp)
        for fp in range(FC // 2):
            wu, B_wu = ring[ring_i % NRING]
            ring_i += 1
            ld("pool", [B_wu], wu[:, :, :], w_up[:, fp * 256:(fp + 1) * 256].rearrange("(c p) n -> p c n", p=128))
            if gi == 0 and fp == 1:
                for f in range(FC):
                    ld("pool", [B_wd], wd[:, f, :], w_down[f * 128:(f + 1) * 128, :])
            for sub_ in range(2):
                f = fp * 2 + sub_
                pi = f % 2
                g_ps = bankv(pi * 2, [512])
                u_ps = bankv(pi * 2 + 1, [512])
                for c in range(DC):
                    mm([BhfT, B_wg], [bB[pi * 2]], g_ps[:, :GT], wg[:, c, f * 128:(f + 1) * 128], hfT[:, c, :GT], c == 0, c == DC - 1)
                for c in range(DC):
                    mm([BhfT, B_wu], [bB[pi * 2 + 1]], u_ps[:, :GT], wu[:, c, sub_ * 128:(sub_ + 1) * 128], hfT[:, c, :GT], c == 0, c == DC - 1)
                sg_, Bsg_ = sgt[pi]
                act([bB[pi * 2]], [Bsg_], out=sg_[:, :GT], in_=g_ps[:, :GT], func=AF.Silu)
                tt("dve", [bB[pi * 2 + 1], Bsg_], [BactT], out=actT[:, f, :GT], in0=u_ps[:, :GT], in1=sg_[:, :GT], op=ALU.mult)
        sA = kb.capture(lambda: c_down(gi))
        sB = kb.capture(lambda: c_norms(gi + 1)) if gi + 1 < len(groups) else []
        kb.replay(interleave_prop(sA, sB))

    kb.emit()
    return nc


_CACHE = {}


def _consts():
    c = np.zeros((128, NCONST), np.float32)
    i = np.arange(128)
    c[:, C_ID:C_ID + 128] = np.eye(128, dtype=np.float32)
    c[:, C_MT:C_MT + 128] = (i[:, None] <= i[None, :]).astype(np.float32)
    c[:, C_LS:C_LS + 128] = (i[None, :] < i[:, None]).astype(np.float32)
    c[:, C_ONE:C_ONE + 128] = 1.0
    j = np.arange(64)
    sameseq = (j[:, None] // TS) == (j[None, :] // TS)
    c[:64, C_MTS:C_MTS + 64] = (sameseq & (j[:, None] <= j[None, :])).astype(np.float32)
    c[:64, C_LSS:C_LSS + 64] = (sameseq & (j[None, :] < j[:, None])).astype(np.float32)
    c[:64, C_SAME:C_SAME + 64] = sameseq.astype(np.float32)
    c[:64, C_SIND:C_SIND + 16] = ((j[:, None] // TS) == np.arange(16)[None, :]).astype(np.float32)
    c[:, C_NH] = -0.5
    c[:64, C_HM] = 1.0
    c[64:, C_HM + 1] = 1.0
    mrow = ((j[None, :] // TS) == np.arange(16)[:, None]).astype(np.float32).reshape(1, 1024)
    c[:, C_MROW:C_MROW + 1024] = mrow
    return c


def kernel(**inputs):
    debug = bool(inputs.pop("_debug", False))
    stop = inputs.pop("_stop", None)
    key = ("nc", debug, str(stop))
    if key not in _CACHE:
        _CACHE[key] = build_program(debug, stop)
    nc = _CACHE[key]
    f = lambda a: np.ascontiguousarray(np.asarray(a, dtype=np.float32))
    consts = _consts()
    shared = {}
    for name in ("ln_mix_pre", "ln_mix_post", "w_in", "gla_gate_w2", "gla_gate_b", "gla_norm_w", "ssd_conv_w", "ssd_conv_b",
                 "ssd_dt_bias", "ssd_A_log", "ssd_D", "ssd_norm_w", "w_out", "ln_xa_pre", "ln_xa_post", "mem_norm_w",
                 "w_xq", "w_xk", "w_xv", "w_xo", "ln_ffn_pre", "ln_ffn_post", "w_gate", "w_up", "w_down"):
        shared[name] = f(inputs[name])[0]
    xp = f(inputs["x_prompt"]); xs = f(inputs["x_sample"]); memp = f(inputs["mem_prompt"])
    sg = f(inputs["state_gla"])[0]; ssm = f(inputs["state_ssm"])[0]; scv = f(inputs["state_conv"])[0]
    ckk = f(inputs["cache_mem_k"])[0]; cvv = f(inputs["cache_mem_v"])[0]
    in_maps = []
    for c in range(8):
        m = dict(shared)
        sl = slice(c * NSQ, (c + 1) * NSQ)
        m["x_p"] = xp[c]
        m["x_s"] = np.ascontiguousarray(xs[sl].reshape(64, D))
        m["mem"] = memp[c]
        m["sgla"] = np.ascontiguousarray(sg[sl])
        m["sssm"] = np.ascontiguousarray(ssm[sl])
        m["sconv"] = np.ascontiguousarray(scv[sl].reshape(NSQ * 3, D))
        m["ck"] = np.ascontiguousarray(ckk[sl].reshape(NSQ, 256, D))
        m["cv"] = np.ascontiguousarray(cvv[sl].reshape(NSQ, 256, D))
        m["consts"] = consts
        in_maps.append(m)
    res = run_bass_kernel_spmd(nc, in_maps, core_ids=list(range(8)))
    rs = res.results
    cat = lambda k: np.stack([np.asarray(r[k], dtype=np.float32) for r in rs], axis=0)
    y_prompt = cat("y_p")
    y_sample = cat("y_s").reshape(128, TS, D)
    gla_prompt = cat("gla_p")[None]
    ssm_prompt = cat("ssm_p").reshape(8, 8, 64, 128)[None]
    conv_prompt = cat("conv_p")[None]
    mk_prompt = cat("mk_p").reshape(8, 256, 4, 256)[None]
    mv_prompt = cat("mv_p").reshape(8, 256, 4, 256)[None]
    gla_sample = cat("gla_s").reshape(128, 4, 64, 128)[None]
    ssm_sample = cat("ssm_s").reshape(128, 8, 64, 128)[None]
    conv_sample = cat("conv_s").reshape(128, 3, D)[None]
    out = (y_prompt, y_sample, gla_prompt, ssm_prompt, conv_prompt, mk_prompt, mv_prompt, gla_sample, ssm_sample, conv_sample)
    if debug:
        return out, {k: cat(k) for k in ("dbg_h1", "dbg_h2")}
    return out
```

```python
import numpy as np
from contextlib import ExitStack
import concourse.bass as bass
import concourse.mybir as mybir
from concourse.bass_utils import run_bass_kernel_spmd

F32 = mybir.dt.float32
BF16 = mybir.dt.bfloat16
ALU = mybir.AluOpType
AF = mybir.ActivationFunctionType
AX = mybir.AxisListType

D = 1024
DC = 8
NT = 16
SEQ = 2048
NSQ = 16
TS = 4
DFF = 2816
FC = 22
INC = 3096
EPS = 1e-6
SAMPLE_SSD_DISABLED = False
O_Q, O_K, O_V, O_G, O_GLR, O_Z, O_XBC, O_DT = 0, 256, 512, 1024, 1536, 1552, 2064, 3088

C_ID, C_MT, C_LS, C_ONE, C_MTS, C_LSS, C_SAME, C_SIND, C_NH, C_MROW = 0, 128, 256, 384, 512, 576, 640, 704, 720, 768
C_HM = 722
NCONST = 768 + 1024


class Buf:
    __slots__ = ("name", "w", "r", "psum")

    def __init__(self, name, psum=False):
        self.name = name
        self.w = None
        self.r = {}
        self.psum = psum


class KB:
    ENGS = ("pe", "act", "dve", "pool", "sp")

    def __init__(self, nc, n_dma_sems=32):
        self.nc = nc
        self.es = ExitStack()
        self.items = {e: [] for e in self.ENGS}
        self.seen = {e: {} for e in self.ENGS}
        self.n_dma_sems = n_dma_sems
        self.dma_cnt = [0] * n_dma_sems
        self.dma_rr = 0
        self.dma_rr_sw = 0
        self.targets = set()
        self.rec = None
        self.glue = False

    def sb(self, name, shape, dtype):
        return self.es.enter_context(self.nc.sbuf_tensor(name, list(shape), dtype))

    def ps(self, name, shape, dtype):
        return self.es.enter_context(self.nc.psum_tensor(name, list(shape), dtype))

    def _deps(self, eng, reads, writes):
        need = {}

        def add(dep, war=False):
            if dep is None:
                return
            s, i = dep
            if s == "pe" and eng == "pe":
                return
            if self.seen[eng].get(s, -1) >= i:
                return
            if need.get(s, -1) < i:
                need[s] = i

        for b in reads:
            add(b.w)
            if b.psum:
                for s, i in b.r.items():
                    add((s, i), war=True)
        for b in writes:
            add(b.w)
            for s, i in b.r.items():
                add((s, i), war=True)
        for s, i in need.items():
            self.seen[eng][s] = i
            if not s.startswith("dma"):
                self.targets.add((s, i))
        return need

    def capture(self, fn):
        old = self.rec
        self.rec = []
        fn()
        r = self.rec
        self.rec = old
        return r

    def replay(self, recs):
        for kind, eng, fn, R_, W_, _g, _c in recs:
            if kind == "op":
                self.op(eng, fn, R_, W_)
            else:
                self.dma(eng, fn, R_, W_)

    def op(self, eng, fn, reads=(), writes=(), cost=0.3):
        if self.rec is not None:
            self.rec.append(("op", eng, fn, tuple(reads), tuple(writes), self.glue, cost))
            return
        need = self._deps(eng, reads, writes)
        idx = len(self.items[eng])
        self.items[eng].append(dict(fn=fn, deps=need, dma=None))
        for b in reads:
            if b.r.get(eng, -1) < idx:
                b.r[eng] = idx
        for b in writes:
            b.w = (eng, idx)
            b.r = {}
        return idx

    def dma(self, eng, fn, reads=(), writes=(), cost=2.5):
        if self.rec is not None:
            self.rec.append(("dma", eng, fn, tuple(reads), tuple(writes), False, cost))
            return
        half = self.n_dma_sems // 2
        if eng == "pool":
            k = half + self.dma_rr_sw
            self.dma_rr_sw = (self.dma_rr_sw + 1) % (self.n_dma_sems - half)
        else:
            k = self.dma_rr
            self.dma_rr = (self.dma_rr + 1) % half
        sname = "dma%d" % k
        need = self._deps(eng, reads, writes)
        prev = self.dma_cnt[k]
        if prev > 0 and self.seen[eng].get(sname, -1) < prev:
            need[sname] = prev
            self.seen[eng][sname] = prev
        self.dma_cnt[k] += 1
        tick = self.dma_cnt[k]
        self.items[eng].append(dict(fn=fn, deps=need, dma=k))
        for b in reads:
            if b.r.get(sname, -1) < tick:
                b.r[sname] = tick
        for b in writes:
            b.w = (sname, tick)
            b.r = {}

    def barrier(self):
        last = {}
        for e in self.ENGS:
            last[e] = -1
            for i in range(len(self.items[e]) - 1, -1, -1):
                if self.items[e][i]["fn"] is not None and self.items[e][i]["dma"] is None:
                    last[e] = i
                    break
        for e in self.ENGS:
            need = {}
            for s, i in last.items():
                if s == e or i < 0:
                    continue
                if self.seen[e].get(s, -1) < i:
                    need[s] = i
                    self.seen[e][s] = i
                    self.targets.add((s, i))
            for k in range(self.n_dma_sems):
                sname = "dma%d" % k
                c = self.dma_cnt[k]
                if c > 0 and self.seen[e].get(sname, -1) < c:
                    need[sname] = c
                    self.seen[e][sname] = c
            self.items[e].append(dict(fn=None, deps=need, dma=None))

    def emit(self):
        nc = self.nc
        es = self.es
        sems = {e: es.enter_context(nc.semaphore("s_" + e)) for e in self.ENGS}
        dsems = [es.enter_context(nc.semaphore("s_dma%d" % k)) for k in range(self.n_dma_sems)]
        incval = {}
        for e in self.ENGS:
            c = 0
            for i, it in enumerate(self.items[e]):
                if (e, i) in self.targets:
                    assert it["fn"] is not None and it["dma"] is None, (e, i)
                    c += 1
                    incval[(e, i)] = c
        items = self.items
        dma_cnt = self.dma_cnt
        targets = self.targets
        n_dma_sems = self.n_dma_sems

        def run(e, engine):
            for i, it in enumerate(items[e]):
                for s, v in it["deps"].items():
                    if s.startswith("dma"):
                        engine.wait_ge(dsems[int(s[3:])], 16 * v)
                    else:
                        engine.wait_ge(sems[s], incval[(s, v)])
                if it["fn"] is None:
                    continue
                ins = it["fn"](engine)
                if it["dma"] is not None:
                    ins.then_inc(dsems[it["dma"]], 16)
                elif (e, i) in targets:
                    ins.then_inc(sems[e], 1)
            if e == "sp":
                for k in range(n_dma_sems):
                    if dma_cnt[k] > 0:
                        engine.wait_ge(dsems[k], 16 * dma_cnt[k])

        with nc.Block() as block:
            @block.tensor
            def _(eng):
                run("pe", eng)

            @block.scalar
            def _(eng):
                run("act", eng)

            @block.vector
            def _(eng):
                run("dve", eng)

            @block.gpsimd
            def _(eng):
                run("pool", eng)

            @block.sync
            def _(eng):
                run("sp", eng)
        es.close()


class Sched:
    LAT = 0.3

    def __init__(self):
        self.eng_free = {}
        self.bw = {}
        self.br = {}

    def _ready(self, rec):
        kind, eng, fn, R_, W_, glue, cost = rec
        t = self.eng_free.get(eng, 0.0)
        for b in R_:
            t = max(t, self.bw.get(id(b), 0.0) + self.LAT)
            if b.psum:
                t = max(t, self.br.get(id(b), 0.0) + self.LAT)
        for b in W_:
            t = max(t, self.bw.get(id(b), 0.0) + self.LAT, self.br.get(id(b), 0.0) + self.LAT)
        return t

    def _commit(self, rec):
        kind, eng, fn, R_, W_, glue, cost = rec
        st_ = self._ready(rec)
        if kind == "dma":
            self.eng_free[eng] = st_ + 0.6
            fin = st_ + cost
        else:
            fin = st_ + cost
            self.eng_free[eng] = fin
        for b in R_:
            if self.br.get(id(b), 0.0) < fin:
                self.br[id(b)] = fin
        for b in W_:
            self.bw[id(b)] = fin
            self.br[id(b)] = 0.0

    def merge(self, *lists):
        lists = [l for l in lists if l]
        pos = [0] * len(lists)
        rem = [sum(r[6] for r in l) for l in lists]
        out = []
        total = sum(len(l) for l in lists)
        while len(out) < total:
            best, bt = None, None
            for i, l in enumerate(lists):
                if pos[i] < len(l):
                    t = self._ready(l[pos[i]])
                    if bt is None or t < bt - 1e-9 or (abs(t - bt) <= 1e-9 and rem[i] > rem[best]):
                        best, bt = i, t
            while True:
                rec = lists[best][pos[best]]
                self._commit(rec)
                out.append(rec)
                rem[best] -= rec[6]
                pos[best] += 1
                if not (pos[best] < len(lists[best]) and lists[best][pos[best]][5]):
                    break
        return out


def interleave_prop(*lists, lead=None):
    if lead is None:
        lead = [0.0] * len(lists)
    lead = [b for l, b in zip(lists, lead) if l]
    lists = [l for l in lists if l]
    pos = [0] * len(lists)
    out = []
    total = sum(len(l) for l in lists)
    tot = [sum(r[6] for r in l) for l in lists]
    cum = [0.0] * len(lists)
    while len(out) < total:
        best, bf = None, None
        for i, l in enumerate(lists):
            if pos[i] < len(l):
                f = (cum[i] + 0.5 * l[pos[i]][6]) / tot[i] - lead[i]
                if bf is None or f < bf:
                    best, bf = i, f
        out.append(lists[best][pos[best]])
        cum[best] += lists[best][pos[best]][6]
        pos[best] += 1
        while pos[best] < len(lists[best]) and lists[best][pos[best]][5]:
            out.append(lists[best][pos[best]])
            cum[best] += lists[best][pos[best]][6]
            pos[best] += 1
    return out


def build_program(debug=False, stop=None):
    nc = bass.Bass("TRN2", target_bir_lowering=False)

    def din(name, shape):
        return nc.dram_tensor(name, list(shape), F32, kind="ExternalInput").ap()

    def dout(name, shape):
        return nc.dram_tensor(name, list(shape), F32, kind="ExternalOutput").ap()

    x_p = din("x_p", [SEQ, D])
    x_s = din("x_s", [64, D])
    mem = din("mem", [256, D])
    sgla = din("sgla", [NSQ, 4, 64, 128])
    sssm = din("sssm", [NSQ, 8, 64, 128])
    sconv = din("sconv", [NSQ * 3, D])
    ck = din("ck", [NSQ, 256, D])
    cv = din("cv", [NSQ, 256, D])
    consts = din("consts", [128, NCONST])
    ln_mix_pre = din("ln_mix_pre", [D]); ln_mix_post = din("ln_mix_post", [D])
    w_in = din("w_in", [D, INC]); gate_w2 = din("gla_gate_w2", [16, 256]); gate_b = din("gla_gate_b", [256])
    gla_nw = din("gla_norm_w", [128]); conv_w = din("ssd_conv_w", [4, D]); conv_b = din("ssd_conv_b", [D])
    dt_bias = din("ssd_dt_bias", [8]); a_log = din("ssd_A_log", [8]); ssd_D = din("ssd_D", [8])
    ssd_nw = din("ssd_norm_w", [512]); w_out = din("w_out", [D, D])
    ln_xa_pre = din("ln_xa_pre", [D]); ln_xa_post = din("ln_xa_post", [D]); mem_nw = din("mem_norm_w", [D])
    w_xq = din("w_xq", [D, D]); w_xk = din("w_xk", [D, D]); w_xv = din("w_xv", [D, D]); w_xo = din("w_xo", [D, D])
    ln_ffn_pre = din("ln_ffn_pre", [D]); ln_ffn_post = din("ln_ffn_post", [D])
    w_gate = din("w_gate", [D, DFF]); w_up = din("w_up", [D, DFF]); w_down = din("w_down", [DFF, D])

    y_p = dout("y_p", [SEQ, D]); y_s = dout("y_s", [64, D])
    gla_p = dout("gla_p", [4, 64, 128]); ssm_p = dout("ssm_p", [512, 128]); conv_p = dout("conv_p", [3, D])
    mk_p = dout("mk_p", [256, D]); mv_p = dout("mv_p", [256, D])
    gla_s = dout("gla_s", [NSQ, 4, 64, 128]); ssm_s = dout("ssm_s", [NSQ, 512, 128]); conv_s = dout("conv_s", [NSQ * 3, D])

    kb = KB(nc)
    sched = Sched()

    def interleave(*lists):
        return sched.merge(*lists)

    h_t = kb.sb("h", [128, NT + 1, D], F32)
    hB = [Buf("h%d" % i) for i in range(NT + 1)]
    identb_t = kb.sb("identb", [128, 128], BF16); B_identb = Buf("identb")
    identb = identb_t[:, :]
    cst = kb.sb("cst", [128, 768], F32); B_cst = Buf("cst")
    gainb = kb.sb("gainb", [128, D], F32); B_gainb = Buf("gainb")
    gcol = kb.sb("gcol", [128, 8], F32); B_gcol = Buf("gcol")
    RBYTES = 135600
    R = kb.sb("R", [128, RBYTES // 2], BF16)
    banks = [kb.ps("bank%d" % i, [128, 512], F32) for i in range(8)]
    bB = [Buf("bank%d" % i, psum=True) for i in range(8)]

    class Alloc:
        def __init__(self):
            self.off = 0

        def reset(self, off=0):
            self.off = off

        def __call__(self, name, shape, dtype, parts=128, at=None):
            n = 1
            for s in shape:
                n *= s
            size = n * (4 if dtype == F32 else 2)
            size = (size + 3) // 4 * 4
            if at is not None:
                off = at[0]
                assert size <= at[1], (name, size, at)
            else:
                off = self.off
                assert off + size <= RBYTES, (name, off, size)
                self.off += size
            self.last = (off, size)
            ap = R[:, off // 2:(off + size) // 2]
            if dtype == F32:
                ap = ap.bitcast(F32)
            ap = ap[:parts, :n]
            if len(shape) == 2:
                ap = ap.rearrange("p (a b) -> p a b", b=shape[1])
            elif len(shape) == 3:
                ap = ap.rearrange("p (a b c) -> p a b c", b=shape[1], c=shape[2])
            if at is not None:
                return ap, at[2]
            return ap, Buf(name)

        def slot(self, name, nbytes):
            off = self.off
            assert off + nbytes <= RBYTES, (name, off, nbytes)
            self.off += nbytes
            return [off, nbytes, Buf(name)]

    def sub(slot, o, n):
        return [slot[0] + o, n, slot[2]]

    alloc = Alloc()

    def bankv(i, shape, dtype=F32, parts=128, off=0):
        n = 1
        for s in shape:
            n *= s
        ap = banks[i][:, :]
        if dtype == BF16:
            ap = ap.bitcast(BF16)
        ap = ap[:parts, off:off + n]
        if len(shape) == 2:
            ap = ap.rearrange("p (a b) -> p a b", b=shape[1])
        elif len(shape) == 3:
            ap = ap.rearrange("p (a b c) -> p a b c", b=shape[1], c=shape[2])
        return ap

    def bc(ap, shape):
        return ap.unsqueeze(len(ap.shape)).to_broadcast(list(shape))

    def _n(ap):
        n = 1
        for d in ap.shape[1:]:
            n *= d
        return n

    def act(R_, W_, **kw):
        kb.op("act", lambda e: e.activation(**kw), R_, W_, cost=0.2 + _n(kw["out"]) / 1400.0)

    def tt(eng, R_, W_, **kw):
        c = (0.1 + _n(kw["out"]) / 900.0) if eng == "dve" else (0.25 + _n(kw["out"]) / 500.0)
        kb.op(eng, lambda e: e.tensor_tensor(**kw), R_, W_, cost=c)

    def ts(eng, R_, W_, **kw):
        c = (0.1 + _n(kw["out"]) / 900.0) if eng == "dve" else (0.25 + _n(kw["out"]) / 500.0)
        kb.op(eng, lambda e: e.tensor_scalar(**kw), R_, W_, cost=c)

    def stt(R_, W_, **kw):
        kb.op("dve", lambda e: e.scalar_tensor_tensor(**kw), R_, W_, cost=0.1 + _n(kw["out"]) / 900.0)

    def cp(eng, R_, W_, out, in_):
        if eng == "act":
            kb.op("act", lambda e: e.copy(out=out, in_=in_), R_, W_, cost=0.2 + _n(out) / 1400.0)
        else:
            kb.op(eng, lambda e: e.tensor_copy(out=out, in_=in_), R_, W_, cost=0.1 + _n(out) / 900.0)

    def mm(R_, W_, out, lhsT, rhs, start=True, stop=True, skip=False):
        n = _n(rhs) * (4 if lhsT.dtype == F32 else 1)
        c = 0.01 + n / 4800.0
        if skip:
            kb.op("pe", lambda e: e.matmul(out, lhsT=lhsT, rhs=rhs, start=start, stop=stop, skip_group_check=True), R_, W_, cost=c)
        else:
            kb.op("pe", lambda e: e.matmul(out, lhsT=lhsT, rhs=rhs, start=start, stop=stop), R_, W_, cost=c)

    def tr(R_, W_, out, in_, ident):
        kb.op("pe", lambda e: e.transpose(out=out, in_=in_, identity=ident), R_, W_, cost=0.035 * (4 if in_.dtype == F32 else 1))

    def ld(eng, W_, out, in_, R_=(), slow=False):
        if slow:
            kb.dma(eng, lambda e: e.dma_start(out=out, in_=in_, allow_slow_non_contiguous=True), R_, W_)
        else:
            kb.dma(eng, lambda e: e.dma_start(out=out, in_=in_), R_, W_)

    def st(R_, out, in_, eng="sp"):
        kb.dma(eng, lambda e: e.dma_start(out=out, in_=in_), R_, ())

    def rowb(vec, n):
        return vec.rearrange("(o n) -> o n", o=1).broadcast_to([128, n])

    def colv(vec, k):
        return vec.rearrange("(k p) -> p k", p=128)

    ld("sp", [B_cst], cst[:, :], consts[:, 0:768])
    cp("dve", [B_cst], [B_identb], identb, cst[:, C_ID:C_ID + 128])
    identf = cst[:, C_ID:C_ID + 128]
    neg_half = cst[:, C_NH:C_NH + 1]
    for ti in range(NT):
        ld("sp", [hB[ti]], h_t[:, ti, :], x_p[ti * 128:(ti + 1) * 128, :])
    ld("sp", [hB[NT]], h_t[:64, NT, :], x_s[:, :])

    def tile_T(ti):
        return 128 if ti < NT else 64

    def norm_T(src, Bsrc, T, dstT, BdstT, col0, scr, bank=2):
        junk, Bjunk, ss, Bss, xnb, Bxnb = scr
        act([Bsrc], [Bjunk, Bss], out=junk[:T, :], in_=src, func=AF.Square, accum_out=ss[:T, 0:1])
        ts("dve", [Bss], [Bss], out=ss[:T, 1:2], in0=ss[:T, 0:1], scalar1=1.0 / D, scalar2=EPS, op0=ALU.mult, op1=ALU.add)
        tt("pool", [Bss, B_cst], [Bss], out=ss[:T, 2:3], in0=ss[:T, 1:2], in1=neg_half[:T, :], op=ALU.pow)
        act([Bsrc, Bss], [Bxnb], out=xnb[:T, :], in_=src, func=AF.Copy, scale=ss[:T, 2:3])
        tp = bankv(bank, [8, 128], BF16)
        for c in range(DC):
            tr([Bxnb, B_identb], [bB[bank]], tp[:, c, :T], xnb[:T, c * 128:(c + 1) * 128], identb[:T, :T])
        tt("dve", [bB[bank], B_gcol], [BdstT], out=dstT[:, :, col0:col0 + T], in0=tp[:, :, :T],
           in1=bc(gcol[:, 0:8], [128, 8, T]), op=ALU.mult)

    def rstd_of(srcs, Bsrcs, T, ss, Bss, junk, Bjunk, ncol, width, scale):
        for j, s in enumerate(srcs):
            act(Bsrcs, [Bjunk, Bss], out=junk[:T, :width], in_=s, func=AF.Square, accum_out=ss[:T, j:j + 1])
        ts("dve", [Bss], [Bss], out=ss[:T, 4:4 + ncol], in0=ss[:T, 0:ncol], scalar1=scale, scalar2=EPS, op0=ALU.mult, op1=ALU.add)
        tt("pool", [Bss, B_cst], [Bss], out=ss[:T, 8:8 + ncol], in0=ss[:T, 4:4 + ncol],
           in1=neg_half[:T, :].to_broadcast([T, ncol]) if ncol > 1 else neg_half[:T, :], op=ALU.pow)

    alloc.reset()
    win, B_win = alloc("win", [8, INC], BF16)
    wout, B_wout = alloc("wout", [8, D], BF16)
    w2, B_w2 = alloc("w2", [256], BF16, parts=16)
    gateb_b, B_gateb = alloc("gateb_b", [256], F32)
    gnw_b, B_gnw = alloc("gnw_b", [128], F32)
    snw_b, B_snw = alloc("snw_b", [512], F32)
    small, B_small = alloc("small", [48], F32)
    cwt, B_cwt = alloc("cwt", [8, 5], F32)
    DmI, B_DmI = alloc("DmI", [8, 128], BF16)
    ssF, BssF = alloc("ssF", [16], F32)
    s_fx = alloc.slot("s_fx", 4096)
    xnb, Bxnb = alloc("xnb", [1024], BF16, at=sub(s_fx, 0, 2048))
    xnT, BxnT = alloc("xnT", [8, 128], BF16, at=sub(s_fx, 2048, 2048))
    c3o, Bc3o = alloc("c3o", [1024], F32, at=s_fx)
    ext, Bext = alloc("ext", [8, 131], F32)
    carry, Bcarry = alloc("carry", [8, 3], F32)
    s_cl = alloc.slot("s_cl", 4096)
    cacc, Bcacc = alloc("cacc", [8, 128], F32, at=s_cl)
    c3, Bc3 = alloc("c3", [8, 48], F32, at=s_cl)
    junkF, BjunkF = alloc("junkF", [1024], BF16, at=s_cl)
    xg, Bxg = alloc("xg", [256], F32, at=sub(s_cl, 0, 1024))
    spl, Bspl = alloc("spl", [256], F32, at=sub(s_cl, 1024, 1024))
    enbT, BenbT = alloc("enbT", [2, 128], F32, at=sub(s_cl, 2048, 1024))
    qtT, BqtT = alloc("qtT", [2, 128], BF16, at=sub(s_cl, 3072, 512))
    glrT, BglrT = alloc("glrT", [128], BF16, parts=16)
    scrF = (junkF, BjunkF, ssF, BssF, xnb, Bxnb)
    HO_SPEC = (("xbcT", [8, 128], BF16, 128), ("vb", [512], BF16, 128),
               ("gsw", [512], F32, 128), ("sz", [512], F32, 128), ("dtt", [80], F32, 128),
               ("ebT", [2, 128], F32, 128), ("ktT", [2, 128], BF16, 128), ("qtTz", [2, 2, 128], BF16, 128), ("ktok", [256], BF16, 128))
    HO = [dict((n, alloc(n + "0", sh, dt_, parts=pp)) for (n, sh, dt_, pp) in HO_SPEC)]
    s_p1 = alloc.slot("s_p1", 10624)
    ho1 = {}
    o1 = 0
    for (n, sh, dt_, pp) in HO_SPEC:
        ho1[n] = alloc(n + "1", sh, dt_, parts=pp, at=[s_p1[0] + o1, 1 << 20, Buf(n + "1")])
        o1 += alloc.last[1]
    HO.append(ho1)
    def _AA(name, shape, dt_, off, parts=128):
        return alloc(name, shape, dt_, parts=parts, at=[s_p1[0] + off, 1 << 20, Buf(name)])
    grot = []
    for r in range(3):
        o_ = r * 3072
        grot.append(dict(sgl=_AA("sgl%d" % r, [2, 128], F32, o_), sglb=_AA("sglb%d" % r, [2, 128], BF16, o_ + 1024),
                         qtTm=_AA("qtTm%d" % r, [2, 2, 64], BF16, o_ + 1536), vm=_AA("vm%d" % r, [512], BF16, o_ + 2048, parts=64)))
    srot = []
    for r in range(2):
        o_ = r * 4928
        srot.append(dict(sss=_AA("sss%d" % r, [4, 128], F32, o_), sssb=_AA("sssb%d" % r, [4, 128], BF16, o_ + 2048),
                         sssT=_AA("sssT%d" % r, [512], BF16, o_ + 3072), CTm=_AA("CTm%d" % r, [2, 64], BF16, o_ + 4096),
                         Bm=_AA("Bm%d" % r, [256], BF16, o_ + 4352, parts=64)))
    ss, Bss = alloc("ss", [16], F32)
    junk, Bjunk = alloc("junk", [512], BF16)
    t1, Bt1 = alloc("t1", [512], F32)
    yv, Byv = alloc("yv", [512], F32)
    s_L = alloc.slot("s_L", 4096)
    Lh, BLh = alloc("Lh", [8, 128], F32, at=s_L)
    t1x, Bt1x = alloc("t1x", [1024], F32, at=s_L)
    mix, Bmix = alloc("mix", [1024], BF16)
    mixT, BmixT = alloc("mixT", [8, 128], BF16)
    s_e = alloc.slot("s_e", 2048)
    Ee, BEe = alloc("Ee", [8, 128], BF16, at=s_e)
    s_m = alloc.slot("s_m", 2048)
    MT, BMT = alloc("MT", [8, 128], BF16, at=s_m)
    s_a = alloc.slot("s_a", 1024)
    attm, Battm = alloc("attm", [4, 128], BF16, at=s_a)
    xdt, Bxdt = alloc("xdt", [512], BF16, at=s_a)
    s_k = alloc.slot("s_k", 1024)
    xtok, Bxtok = alloc("xtok", [512], BF16, at=s_k)
    s_zx = alloc.slot("s_zx", 1024)
    xw, Bxw = alloc("xw", [512], BF16, at=s_zx)
    s_kc = alloc.slot("s_kc", 512)
    cbm, Bcbm = alloc("cbm", [2, 128], BF16, at=s_kc)
    Btok, BBtok = alloc("Btok", [256], BF16)
    s_Sg = alloc.slot("s_Sg", 2048)
    Sg, BSg = alloc("Sg", [2, 256], F32, at=s_Sg)
    dtaB, BdtaB = alloc("dtaB", [512], F32, at=s_Sg)
    Sgb, BSgb = alloc("Sgb", [2, 256], BF16)
    SsT, BSsT = alloc("SsT", [512], F32)
    SsTb, BSsTb = alloc("SsTb", [512], BF16)
    ellT, BellT = alloc("ellT", [4, 16], F32)

    WIN_GROUPS = [(0, INC)]
    B_winG = [Buf("win_g%d" % k) for k in range(len(WIN_GROUPS))]

    def winB(col):
        for k, (a_, b_) in enumerate(WIN_GROUPS):
            if a_ <= col < b_:
                return B_winG[k]
    for k, (a_, b_) in enumerate(WIN_GROUPS):
        for c in range(DC):
            ld("pool", [B_winG[k]], win[:, c, a_:b_], w_in[c * 128:(c + 1) * 128, a_:b_])
    ld("sp", [B_gcol], gcol[:, :], colv(ln_mix_pre, 8), slow=True)
    ld("sp", [B_gainb], gainb[:, :], rowb(ln_mix_post, D))
    ld("pool", [B_w2], w2[:, :], gate_w2[:, :])
    ld("sp", [B_gateb], gateb_b[:, :], rowb(gate_b, 256))
    ld("sp", [B_gnw], gnw_b[:, :], rowb(gla_nw, 128))
    ld("sp", [B_snw], snw_b[:, :], rowb(ssd_nw, 512))
    ld("sp", [B_small], small[:, 0:8], rowb(dt_bias, 8))
    ld("sp", [B_small], small[:, 8:16], rowb(a_log, 8))
    ld("sp", [B_small], small[:, 16:24], rowb(ssd_D, 8))
    for j in range(4):
        ld("sp", [B_cwt], cwt[:, :, j], colv(conv_w[j, :], 8), slow=True)
    ld("sp", [B_cwt], cwt[:, :, 4], colv(conv_b, 8), slow=True)
    for c in range(DC):
        ld("pool", [B_wout], wout[:, c, :], w_out[c * 128:(c + 1) * 128, :])
    act([B_small], [B_small], out=small[:, 24:32], in_=small[:, 8:16], func=AF.Exp)
    ts("dve", [B_small], [B_small], out=small[:, 24:32], in0=small[:, 24:32], scalar1=-1.0, scalar2=None, op0=ALU.mult)
    for hh in range(8):
        ts("dve", [B_cst, B_small], [B_DmI], out=DmI[:, hh, :], in0=identf, scalar1=small[:, 16 + hh:17 + hh], scalar2=None, op0=ALU.mult)
    kb.op("pool", lambda e: e.memset(Sg[:, :, :], 0.0), (), [BSg])
    kb.op("pool", lambda e: e.memset(Sgb[:, :, :], 0.0), (), [BSgb])
    kb.op("pool", lambda e: e.memset(SsT[:, :], 0.0), (), [BSsT])
    kb.op("pool", lambda e: e.memset(SsTb[:, :], 0.0), (), [BSsTb])
    kb.op("pool", lambda e: e.memset(carry[:, :, :], 0.0), (), [Bcarry])

    maskT_p = cst[:, C_MT:C_MT + 128]
    Ls_p = cst[:, C_LS:C_LS + 128]
    ones_p = cst[:, C_ONE:C_ONE + 128]
    maskT_s = cst[:64, C_MTS:C_MTS + 64]
    Ls_s = cst[:64, C_LSS:C_LSS + 64]
    same_s = cst[:64, C_SAME:C_SAME + 64]
    sind = cst[:64, C_SIND:C_SIND + 16]

    def a_front(ti, p):
        sample = ti == NT
        T = 64 if sample else 128
        nseq = NSQ if sample else 1
        Tq = TS if sample else 128
        H = HO[p]
        xbcT, BxbcT = H["xbcT"]; vb, Bvb = H["vb"]
        gsw, Bgsw = H["gsw"]; sz, Bsz = H["sz"]; dtt, Bdtt = H["dtt"]
        ebT, BebT = H["ebT"]; ktT, BktT = H["ktT"]; qtTz, BqtTz = H["qtTz"]; ktok, Bktok = H["ktok"]
        maskT = maskT_s if sample else maskT_p
        same = same_s if sample else ones_p
        extv = ext[:, :, :nseq * (3 + Tq)].rearrange("p c (b t) -> p c b t", t=3 + Tq)
        if sample:
            ld("sp", [Bc3o], c3o[:48, :], sconv[:, :])
            c3ps = bankv(0, [8, 48])
            for c in range(DC):
                tr([Bc3o, B_cst], [bB[0]], c3ps[:, c, :], c3o[:48, c * 128:(c + 1) * 128], identf[:48, :48])
            cp("dve", [bB[0]], [Bext], extv[:, :, :, 0:3], c3ps[:, :, :].rearrange("p c (b j) -> p c b j", j=3))
        norm_T(h_t[:T, ti, :], hB[ti], T, xnT, BxnT, 0, scrF, bank=0)

        def proj_fm(bank, nj, col0):
            ps_ = bankv(bank, [4, 128])
            for j in range(nj):
                for c in range(DC):
                    mm([winB(col0 + j * 128), BxnT], [bB[bank]], ps_[:, j, :T], win[:, c, col0 + j * 128:col0 + (j + 1) * 128], xnT[:, c, :T], c == 0, c == DC - 1)
            return ps_

        def proj_tm(bank, col0):
            ps_ = bankv(bank, [512], parts=T)
            for c in range(DC):
                mm([winB(col0), BxnT], [bB[bank]], ps_, xnT[:, c, :T], win[:, c, col0:col0 + 512], c == 0, c == DC - 1)
            return ps_

        extv = ext[:, :, :nseq * (3 + Tq)].rearrange("p c (b t) -> p c b t", t=3 + Tq)
        qk_ps = proj_fm(1, 4, O_Q)
        glr_ps = bankv(2, [128], parts=16)
        for c in range(DC):
            mm([winB(O_GLR), BxnT], [bB[2]], glr_ps[:, :T], win[:, c, O_GLR:O_GLR + 16], xnT[:, c, :T], c == 0, c == DC - 1)
        dt_ps = bankv(2, [8], parts=T, off=256)
        for c in range(DC):
            mm([winB(O_DT), BxnT], [bB[2]], dt_ps, xnT[:, c, :T], win[:, c, O_DT:O_DT + 8], c == 0, c == DC - 1)
        x0_ps = proj_fm(3, 4, O_XBC)
        cp("act", [bB[2]], [BglrT], glrT[:, :T], glr_ps[:, :T])
        tt("dve", [bB[2], B_small], [Bdtt], out=dtt[:T, 0:8], in0=dt_ps, in1=small[:T, 0:8], op=ALU.add)
        act([Bdtt], [Bdtt], out=dtt[:T, 8:16], in_=dtt[:T, 0:8], func=AF.Exp)
        act([Bdtt], [Bdtt], out=dtt[:T, 16:24], in_=dtt[:T, 8:16], func=AF.Ln, bias=1.0, scale=1.0)
        tt("dve", [Bdtt, B_small], [Bdtt], out=dtt[:T, 24:32], in0=dtt[:T, 16:24], in1=small[:T, 24:32], op=ALU.mult)
        lc_ps = bankv(2, [16], parts=T, off=320)
        mm([B_cst, Bdtt], [bB[2]], lc_ps[:, 0:8], maskT, dtt[:T, 24:32])
        mm([B_cst, Bdtt], [bB[2]], lc_ps[:, 8:16], same, dtt[:T, 24:32])
        cp("act", [bB[2]], [Bdtt], dtt[:T, 64:80], lc_ps[:, 0:16])
        act([Bdtt], [Bdtt], out=dtt[:T, 32:40], in_=dtt[:T, 64:72], func=AF.Exp)
        tt("dve", [Bdtt], [Bdtt], out=dtt[:T, 56:64], in0=dtt[:T, 72:80], in1=dtt[:T, 64:72], op=ALU.subtract)
        act([Bdtt], [Bdtt], out=dtt[:T, 40:48], in_=dtt[:T, 56:64], func=AF.Exp)
        act([Bdtt], [Bdtt], out=dtt[:T, 48:56], in_=dtt[:T, 72:80], func=AF.Exp)
        if not sample:
            cp("dve", [Bcarry], [Bext], extv[:, :, 0, 0:3], carry[:, :, :])
        cp("act", [bB[3]], [Bext], extv[:, 0:4, :, 3:3 + Tq], x0_ps[:, :, :T].rearrange("p c (b t) -> p c b t", t=Tq))
        lg_ps = bankv(0, [256], parts=T)
        mm([BglrT, B_w2], [bB[0]], lg_ps, glrT[:, :T], w2[:, :])
        tt("dve", [bB[0], B_gateb], [Bxg], out=xg[:T, :], in0=lg_ps, in1=gateb_b[:T, :], op=ALU.add)
        act([Bxg], [Bspl], out=spl[:T, :], in_=xg[:T, :], func=AF.Exp, scale=-1.0)
        act([Bspl], [Bspl], out=spl[:T, :], in_=spl[:T, :], func=AF.Ln, bias=1.0, scale=1.0)
        v_ps = proj_tm(2, O_V)
        cp("act", [bB[2]], [Bvb], vb[:T, :], v_ps)
        g_ps = proj_tm(3, O_G)
        bT_ps = bankv(0, [2, 128], off=256)
        for c in range(2):
            mm([Bspl, B_cst], [bB[0]], bT_ps[:, c, :T], spl[:T, c * 128:(c + 1) * 128], maskT)
        act([bB[0]], [BebT], out=ebT[:, :, :T], in_=bT_ps[:, :, :T], func=AF.Exp, scale=-1.0 / 16.0)
        act([bB[0]], [BenbT], out=enbT[:, :, :T], in_=bT_ps[:, :, :T], func=AF.Exp, scale=1.0 / 16.0)
        stt([bB[1], BebT], [BqtT], out=qtT[:, :, :T], in0=qk_ps[:, 0:2, :T], scalar=0.125, in1=ebT[:, :, :T], op0=ALU.mult, op1=ALU.mult)
        tt("dve", [bB[1], BenbT], [BktT], out=ktT[:, :, :T], in0=qk_ps[:, 2:4, :T], in1=enbT[:, :, :T], op=ALU.mult)
        for i in range(2):
            ts("dve", [BqtT, B_cst], [BqtTz], out=qtTz[:, :, i, :T], in0=qtT[:, :, :T], scalar1=cst[:, C_HM + i:C_HM + i + 1], scalar2=None, op0=ALU.mult)
        ktr_ps = bankv(0, [256], BF16, parts=T)
        for c in range(2):
            tr([BktT, B_identb], [bB[0]], ktr_ps[:, c * 128:(c + 1) * 128], ktT[:, c, :T], identb)
        cp("act", [bB[0]], [Bktok], ktok[:T, :], ktr_ps)
        x1_ps = proj_fm(1, 4, O_XBC + 512)
        cp("act", [bB[1]], [Bext], extv[:, 4:8, :, 3:3 + Tq], x1_ps[:, :, :T].rearrange("p c (b t) -> p c b t", t=Tq))
        z_ps = proj_tm(0, O_Z)
        if not sample:
            cp("dve", [Bext], [Bcarry], carry[:, :, :], extv[:, :, 0, Tq:Tq + 3])
        last_tile_of_seq = sample or ti == NT - 1
        if last_tile_of_seq:
            n3 = nseq * 3
            cp("dve", [Bext], [Bc3], c3[:, :, :n3].rearrange("p c (b j) -> p c b j", j=3), extv[:, :, :, Tq:Tq + 3])
            cops = [bankv(1, [512], parts=n3), bankv(2, [512], parts=n3)]
            cbk = [1, 2]
            for c in range(DC):
                tr([Bc3, B_cst], [bB[cbk[c // 4]]], cops[c // 4][:, (c % 4) * 128:(c % 4 + 1) * 128], c3[:, c, :n3], identf)
            for half in range(2):
                cp("dve", [bB[cbk[half]]], [Bc3o], c3o[:n3, half * 512:(half + 1) * 512], cops[half])
            st([Bc3o], (conv_s if sample else conv_p)[:, :], c3o[:n3, :])
        caccv = cacc[:, :, :T].rearrange("p c (b t) -> p c b t", t=Tq)
        for c in range(DC):
            ts("dve", [Bext, B_cwt], [Bcacc], out=caccv[:, c], in0=extv[:, c, :, 3:3 + Tq], scalar1=cwt[:, c, 3:4], scalar2=cwt[:, c, 4:5],
               op0=ALU.mult, op1=ALU.add)
            for j in range(3):
                stt([Bext, B_cwt, Bcacc], [Bcacc], out=caccv[:, c], in0=extv[:, c, :, j:j + Tq], scalar=cwt[:, c, j:j + 1], in1=caccv[:, c],
                    op0=ALU.mult, op1=ALU.add)
        act([bB[3]], [Bgsw], out=gsw[:T, :], in_=g_ps, func=AF.Silu)
        kb.glue = True
        act([bB[0]], [Bsz], out=sz[:T, :], in_=z_ps, func=AF.Silu)
        act([Bcacc], [BxbcT], out=xbcT[:, :, :T], in_=cacc[:, :, :T], func=AF.Silu)
        kb.glue = False
        tt("pool", [Bgsw, B_gnw], [Bgsw], out=gsw[:T, :].rearrange("p (h v) -> p h v", v=128), in0=gsw[:T, :].rearrange("p (h v) -> p h v", v=128),
           in1=gnw_b[:T, :].unsqueeze(1).to_broadcast([T, 4, 128]), op=ALU.mult)

    def a_back(ti, p):
        sample = ti == NT
        T = 64 if sample else 128
        H = HO[p]
        xbcT, BxbcT = H["xbcT"]; vb, Bvb = H["vb"]
        gsw, Bgsw = H["gsw"]; sz, Bsz = H["sz"]; dtt, Bdtt = H["dtt"]
        ebT, BebT = H["ebT"]; ktT, BktT = H["ktT"]; qtTz, BqtTz = H["qtTz"]; ktok, Bktok = H["ktok"]
        maskT = maskT_s if sample else maskT_p
        Ls = Ls_s if sample else Ls_p
        same = same_s if sample else ones_p
        att_ps = bankv(6, [4, 128], parts=T)
        for hh in range(4):
            mm([BktT, BqtTz], [bB[6]], att_ps[:, hh, :T], ktT[:, hh // 2, :T], qtTz[:, hh // 2, hh % 2, :T])
        tt("dve", [bB[6], B_cst], [Battm], out=attm[:T, :, :T], in0=att_ps[:, :, :T],
           in1=maskT.unsqueeze(1).to_broadcast([T, 4, T]), op=ALU.mult)
        o_ps = bankv(7, [512], parts=T)
        for hh in range(4):
            c = hh // 2
            mm([Battm, Bvb], [bB[7]], o_ps[:, hh * 128:(hh + 1) * 128], attm[:T, hh, :T], vb[:T, hh * 128:(hh + 1) * 128],
               (hh == 0) if sample else True, False, skip=sample)
            if not sample:
                mm([BqtTz, BSgb], [bB[7]], o_ps[:, hh * 128:(hh + 1) * 128], qtTz[:, c, hh % 2, :T],
                   Sgb[:, c, (hh % 2) * 128:(hh % 2 + 1) * 128], False, True)
        if not sample:
            Pg_ps = bankv(4, [2, 256])
            for c in range(2):
                mm([Bktok, Bvb], [bB[4]], Pg_ps[:, c, :], ktok[:T, c * 128:(c + 1) * 128], vb[:T, c * 256:(c + 1) * 256])
            tt("dve", [bB[4], BSg], [BSg], out=Sg[:, :, :], in0=Pg_ps[:, :, :], in1=Sg[:, :, :], op=ALU.add)
            tt("dve", [BSg, BebT], [BSg], out=Sg[:, :, :], in0=Sg[:, :, :], in1=bc(ebT[:, :, T - 1], [128, 2, 256]), op=ALU.mult)
            cp("act", [BSg], [BSgb], Sgb[:, :, :], Sg[:, :, :])
            if ti == NT - 1:
                for hh in range(4):
                    c, r0 = hh // 2, (hh % 2) * 64
                    st([BSg], gla_p[hh, :, :], Sg[r0:r0 + 64, c, (hh % 2) * 128:(hh % 2 + 1) * 128])
        else:
            NR = 3
            for r in range(NR):
                kb.op("pool", lambda e, r=r: e.memset(grot[r]["qtTm"][0][:, :, :, :], 0.0), (), [grot[r]["qtTm"][1]])

            def gla_load(b):
                sgl, Bsgl = grot[b % NR]["sgl"]
                ld("sp", [Bsgl], sgl[:, :, :], sgla[b].rearrange("(c i) k v -> (i k) c v", i=2))
            for b in range(min(NR - 1, NSQ)):
                gla_load(b)
            for b in range(NSQ):
                if b + NR - 1 < NSQ:
                    gla_load(b + NR - 1)
                R_ = grot[b % NR]
                (sgl, Bsgl), (sglb, Bsglb), (qtTm, BqtTm), (vm, Bvm) = R_["sgl"], R_["sglb"], R_["qtTm"], R_["vm"]
                c0 = b * TS
                if b >= NR:
                    pc = (b - NR) * TS
                    kb.op("pool", lambda e, qtTm=qtTm, pc=pc: e.memset(qtTm[:, :, :, pc:pc + TS], 0.0), (), [BqtTm])
                kb.op("dve", lambda e, qtTm=qtTm, c0=c0: e.tensor_copy(out=qtTm[:, :, :, c0:c0 + TS], in_=qtTz[:, :, :, c0:c0 + TS]), [BqtTz], [BqtTm])
                ts("dve", [Bvb, B_cst], [Bvm], out=vm[:64, :], in0=vb[:64, :], scalar1=sind[:, b:b + 1], scalar2=None, op0=ALU.mult)
                cp("act", [Bsgl], [Bsglb], sglb[:, :, :], sgl[:, :, :])
                for hh in range(4):
                    c = hh // 2
                    mm([BqtTm, Bsglb], [bB[7]], o_ps[:, hh * 128:(hh + 1) * 128], qtTm[:, c, hh % 2, :],
                       sglb[:, c, :], False, b == NSQ - 1, skip=True)
                Pg_ps = bankv(4, [2, 256])
                for c in range(2):
                    mm([Bktok, Bvm], [bB[4]], Pg_ps[:, c, :], ktok[:64, c * 128:(c + 1) * 128], vm[:64, c * 256:(c + 1) * 256])
                for i in range(2):
                    tt("dve", [bB[4], Bsgl], [Bsgl], out=sgl[i * 64:(i + 1) * 64, :, :], in0=Pg_ps[i * 64:(i + 1) * 64, :, i * 128:(i + 1) * 128],
                       in1=sgl[i * 64:(i + 1) * 64, :, :], op=ALU.add)
                tt("dve", [Bsgl, BebT], [Bsgl], out=sgl[:, :, :], in0=sgl[:, :, :], in1=bc(ebT[:, :, b * TS + TS - 1], [128, 2, 128]), op=ALU.mult)
                st([Bsgl], gla_s[b].rearrange("(c i) k v -> (i k) c v", i=2), sgl[:, :, :], eng="pool")
        rstd_of([o_ps[:, hh * 128:(hh + 1) * 128] for hh in range(4)], [bB[7]], T, ss, Bss, junk, Bjunk, 4, 128, 1.0 / 128)
        tt("dve", [bB[7], Bss], [Bt1], out=t1[:T, :].rearrange("p (h v) -> p h v", v=128), in0=o_ps.rearrange("p (h v) -> p h v", v=128),
           in1=bc(ss[:T, 8:12], [T, 4, 128]), op=ALU.mult)
        tt("dve", [Bt1, Bgsw], [Bmix], out=mix[:T, 0:512], in0=t1[:T, :], in1=gsw[:T, :], op=ALU.mult)
        if sample:
            kb.barrier()
        BT_ = xbcT[:, 4:6, :]
        CT_ = xbcT[:, 6:8, :]
        tt("dve", [B_cst, Bdtt], [BLh], out=Lh[:T, :, :T], in0=Ls.unsqueeze(1).to_broadcast([T, 8, T]),
           in1=bc(dtt[:T, 24:32], [T, 8, T]), op=ALU.mult)
        segb = [6, 4]
        seg_ps = [bankv(6, [4, 128], parts=T), bankv(4, [4, 128], parts=T)]
        for hh in range(8):
            mm([BLh, B_cst], [bB[segb[hh // 4]]], seg_ps[hh // 4][:, hh % 4, :T], Lh[:T, hh, :T], maskT)
        for half in range(2):
            act([bB[segb[half]]], [BEe], out=Ee[:T, half * 4:(half + 1) * 4, :T], in_=seg_ps[half][:, :, :T], func=AF.Exp)
        cb_ps = bankv(5, [2, 128], parts=T, off=256)
        for g in range(2):
            mm([BxbcT], [bB[5]], cb_ps[:, g, :T], BT_[:, g, :T], CT_[:, g, :T])
        tt("dve", [bB[5], B_cst], [Bcbm], out=cbm[:T, :, :T], in0=cb_ps[:, :, :T], in1=maskT.unsqueeze(1).to_broadcast([T, 2, T]), op=ALU.mult)
        for g in range(2):
            tt("dve", [Bcbm, BEe], [BMT], out=MT[:T, g * 4:(g + 1) * 4, :T], in0=cbm[:T, g, :T].unsqueeze(1).to_broadcast([T, 4, T]),
               in1=Ee[:T, g * 4:(g + 1) * 4, :T], op=ALU.mult)
        xtr_ps = bankv(7, [768], BF16, parts=T)
        for c in range(6):
            tr([BxbcT, B_identb], [bB[7]], xtr_ps[:, c * 128:(c + 1) * 128], xbcT[:, c, :T], identb)
        tt("dve", [bB[7], Bdtt], [Bxdt], out=xdt[:T, :].rearrange("p (h q) -> p h q", q=64), in0=xtr_ps[:, 0:512].rearrange("p (h q) -> p h q", q=64),
           in1=bc(dtt[:T, 16:24], [T, 8, 64]), op=ALU.mult)
        cp("act", [bB[7]], [Bxtok], xtok[:T, :], xtr_ps[:, 0:512])
        cp("act", [bB[7]], [BBtok], Btok[:T, :], xtr_ps[:, 512:768])
        tt("dve", [Bxdt, Bdtt], [Bxw], out=xw[:T, :].rearrange("p (h q) -> p h q", q=64), in0=xdt[:T, :].rearrange("p (h q) -> p h q", q=64),
           in1=bc(dtt[:T, 40:48], [T, 8, 64]), op=ALU.mult)
        y_ps = bankv(6, [512], parts=T)
        for hh in range(8):
            mm([BMT, Bxdt], [bB[6]], y_ps[:, hh * 64:(hh + 1) * 64], MT[:T, hh, :T], xdt[:T, hh * 64:(hh + 1) * 64], True, False)
            mm([B_DmI, Bxtok], [bB[6]], y_ps[:, hh * 64:(hh + 1) * 64], DmI[:T, hh, :T], xtok[:T, hh * 64:(hh + 1) * 64], False, True)
        yi_ps = bankv(4, [512], parts=T)
        if not sample:
            for g in range(2):
                mm([BxbcT, BSsTb], [bB[4]], yi_ps[:, g * 256:(g + 1) * 256], CT_[:, g, :T], SsTb[:, g * 256:(g + 1) * 256])
            PT_ps = bankv(5, [512])
            for g in range(2):
                mm([BBtok, Bxw], [bB[5]], PT_ps[:, g * 256:(g + 1) * 256], Btok[:T, g * 128:(g + 1) * 128], xw[:T, g * 256:(g + 1) * 256])
        else:
            cp("dve", [Bdtt], [BdtaB], dtaB[:64, :].rearrange("p (h q) -> p h q", q=64), bc(dtt[:64, 24:32], [64, 8, 64]))
            ellT_ps = bankv(5, [4, 16])
            for c in range(4):
                mm([BdtaB, B_cst], [bB[5]], ellT_ps[:, c, :], dtaB[:64, c * 128:(c + 1) * 128], sind)
            act([bB[5]], [BellT], out=ellT[:, :, :], in_=ellT_ps[:, :, :], func=AF.Exp)
            for r in range(2):
                kb.op("pool", lambda e, r=r: e.memset(srot[r]["CTm"][0][:, :, :], 0.0), (), [srot[r]["CTm"][1]])
            def ssd_load(b):
                sss, Bsss = srot[b % 2]["sss"]
                ld("sp", [Bsss], sss[:, :, :], sssm[b].rearrange("h q n -> (h q) n").rearrange("(c p) n -> p c n", p=128))
            ssd_load(0)
            for b in range(NSQ):
                if b + 1 < NSQ:
                    ssd_load(b + 1)
                R_ = srot[b % 2]
                (sss, Bsss), (sssb, Bsssb), (sssT, BsssT), (CTm, BCTm), (Bm, BBm) = R_["sss"], R_["sssb"], R_["sssT"], R_["CTm"], R_["Bm"]
                c0 = b * TS
                if b >= 2:
                    pc = (b - 2) * TS
                    kb.op("pool", lambda e, CTm=CTm, pc=pc: e.memset(CTm[:, :, pc:pc + TS], 0.0), (), [BCTm])
                kb.op("dve", lambda e, CTm=CTm, c0=c0: e.tensor_copy(out=CTm[:, :, c0:c0 + TS], in_=CT_[:, :, c0:c0 + TS]), [BxbcT], [BCTm])
                ts("dve", [BBtok, B_cst], [BBm], out=Bm[:64, :], in0=Btok[:64, :], scalar1=sind[:, b:b + 1], scalar2=None, op0=ALU.mult)
                cp("act", [Bsss], [Bsssb], sssb[:, :, :], sss[:, :, :])
                sT_ps = bankv(5, [512], BF16)
                for c in range(4):
                    tr([Bsssb, B_identb], [bB[5]], sT_ps[:, c * 128:(c + 1) * 128], sssb[:, c, :], identb)
                cp("act", [bB[5]], [BsssT], sssT[:, :], sT_ps)
                for g in range(2):
                    mm([BCTm, BsssT], [bB[4]], yi_ps[:, g * 256:(g + 1) * 256], CTm[:, g, :], sssT[:, g * 256:(g + 1) * 256],
                       b == 0 and g == 0, b == NSQ - 1, skip=True)
                Ps_ps = bankv(7, [4, 128])
                for c in range(4):
                    mm([Bxw, BBm], [bB[7]], Ps_ps[:, c, :], xw[:64, c * 128:(c + 1) * 128], Bm[:64, (c // 2) * 128:(c // 2 + 1) * 128])
                tt("dve", [Bsss, BellT], [Bsss], out=sss[:, :, :], in0=sss[:, :, :], in1=bc(ellT[:, :, b], [128, 4, 128]), op=ALU.mult)
                tt("dve", [bB[7], Bsss], [Bsss], out=sss[:, :, :], in0=Ps_ps[:, :, :], in1=sss[:, :, :], op=ALU.add)
                st([Bsss], ssm_s[b].rearrange("(c p) n -> p c n", p=128), sss[:, :, :], eng="pool")
        tt("dve", [bB[4], Bdtt], [Bt1], out=t1[:T, :].rearrange("p (h q) -> p h q", q=64), in0=yi_ps.rearrange("p (h q) -> p h q", q=64),
           in1=bc(dtt[:T, 32:40], [T, 8, 64]), op=ALU.mult)
        tt("dve", [bB[6], Bt1], [Byv], out=yv[:T, :], in0=y_ps, in1=t1[:T, :], op=ALU.add)
        tt("dve", [Byv, Bsz], [Byv], out=yv[:T, :], in0=yv[:T, :], in1=sz[:T, :], op=ALU.mult)
        if not sample:
            tt("dve", [BSsT, Bdtt], [BSsT], out=SsT[:, :].rearrange("p (h q) -> p h q", q=64), in0=SsT[:, :].rearrange("p (h q) -> p h q", q=64),
               in1=bc(dtt[:, 48:56], [128, 8, 64]), op=ALU.mult)
            tt("dve", [bB[5], BSsT], [BSsT], out=SsT[:, :], in0=PT_ps, in1=SsT[:, :], op=ALU.add)
            cp("act", [BSsT], [BSsTb], SsTb[:, :], SsT[:, :])
            if ti == NT - 1:
                fin_ps = bankv(7, [4, 128])
                for c in range(4):
                    tr([BSsT, B_cst], [bB[7]], fin_ps[:, c, :], SsT[:, c * 128:(c + 1) * 128], identf)
                cp("dve", [bB[7]], [Bt1], t1[:, :].rearrange("p (c n) -> p c n", n=128), fin_ps[:, :, :])
                st([Bt1], ssm_p.rearrange("(c p) n -> p c n", p=128), t1[:, :].rearrange("p (c n) -> p c n", n=128))
        rstd_of([yv[:T, g * 256:(g + 1) * 256] for g in range(2)], [Byv], T, ss, Bss, junk, Bjunk, 2, 256, 1.0 / 256)
        for g in range(2):
            stt([Byv, Bss, B_snw], [Bmix], out=mix[:T, 512 + g * 256:512 + (g + 1) * 256], in0=yv[:T, g * 256:(g + 1) * 256],
                scalar=ss[:T, 8 + g:9 + g], in1=snw_b[:T, g * 256:(g + 1) * 256], op0=ALU.mult, op1=ALU.mult)
        mT_ps = bankv(7, [8, 128], BF16)
        for c in range(DC):
            tr([Bmix, B_identb], [bB[7]], mT_ps[:, c, :T], mix[:T, c * 128:(c + 1) * 128], identb[:T, :T])
        cp("act", [bB[7]], [BmixT], mixT[:, :, :T], mT_ps[:, :, :T])
        mob = [6, 4]
        mo_ps = [bankv(6, [512], parts=T), bankv(4, [512], parts=T)]
        for half in range(2):
            for c in range(DC):
                mm([BmixT, B_wout], [bB[mob[half]]], mo_ps[half], mixT[:, c, :T], wout[:, c, half * 512:(half + 1) * 512], c == 0, c == DC - 1)
        post_norm_residual(mo_ps, [bB[6], bB[4]], T, ti, ss, Bss, junk, Bjunk, t1x, Bt1x)

    def post_norm_residual(ps2, Bps2, T, ti, ss, Bss, junk, Bjunk, tmp, Btmp):
        for half in range(2):
            act([Bps2[half]], [Bjunk, Bss], out=junk[:T, :512], in_=ps2[half], func=AF.Square, accum_out=ss[:T, half:half + 1])
        tt("dve", [Bss], [Bss], out=ss[:T, 3:4], in0=ss[:T, 0:1], in1=ss[:T, 1:2], op=ALU.add)
        ts("dve", [Bss], [Bss], out=ss[:T, 4:5], in0=ss[:T, 3:4], scalar1=1.0 / D, scalar2=EPS, op0=ALU.mult, op1=ALU.add)
        tt("pool", [Bss, B_cst], [Bss], out=ss[:T, 8:9], in0=ss[:T, 4:5], in1=neg_half[:T, :], op=ALU.pow)
        if tmp.shape[-1] >= 1024:
            for half in range(2):
                stt([Bps2[half], Bss, B_gainb], [Btmp], out=tmp[:T, half * 512:(half + 1) * 512], in0=ps2[half], scalar=ss[:T, 8:9],
                    in1=gainb[:T, half * 512:(half + 1) * 512], op0=ALU.mult, op1=ALU.mult)
            tt("pool", [Btmp, hB[ti]], [hB[ti]], out=h_t[:T, ti, :], in0=h_t[:T, ti, :], in1=tmp[:T, :], op=ALU.add)
        else:
            for half in range(2):
                stt([Bps2[half], Bss, B_gainb], [Btmp], out=tmp[:T, :512], in0=ps2[half], scalar=ss[:T, 8:9],
                    in1=gainb[:T, half * 512:(half + 1) * 512], op0=ALU.mult, op1=ALU.mult)
                tt("pool", [Btmp, hB[ti]], [hB[ti]], out=h_t[:T, ti, half * 512:(half + 1) * 512], in0=h_t[:T, ti, half * 512:(half + 1) * 512],
                   in1=tmp[:T, :512], op=ALU.add)

    tilesA = list(range(NT + 1))
    if stop is not None and stop[0] == "A":
        tilesA = stop[1]
    if stop is not None and stop[0] == "X":
        tilesA = stop[3]
    ptA = [t for t in tilesA if t < NT]
    nA = len(ptA)
    for i in range(nA + 1):
        streams = []
        if i < nA:
            streams.append(kb.capture(lambda: a_front(ptA[i], i % 2)))
        if 0 <= i - 1 < nA:
            streams.append(kb.capture(lambda: a_back(ptA[i - 1], (i - 1) % 2)))
        kb.replay(interleave_prop(*streams))
    if NT in tilesA:
        kb.barrier()
        a_front(NT, 0)
        a_back(NT, 0)
    if stop is not None and stop[0] == "A":
        kb.emit()
        return nc

    if debug:
        dbg_h1 = dout("dbg_h1", [SEQ + 64, D])
        for ti in range(NT + 1):
            T = tile_T(ti)
            st([hB[ti]], dbg_h1[ti * 128:ti * 128 + T, :], h_t[:T, ti, :])

    kb.barrier()
    alloc.reset()
    wxq, B_wxq = alloc("wxq", [8, D], BF16)
    wxo, B_wxo = alloc("wxo", [8, D], BF16)
    s_wxk = alloc.slot("s_wxk", 16384)
    s_wxv = alloc.slot("s_wxv", 16384)
    wxk, B_wxk = alloc("wxk", [8, D], BF16, at=s_wxk)
    wxv, B_wxv = alloc("wxv", [8, D], BF16, at=s_wxv)
    junk, Bjunk = alloc("junkB", [1024], BF16)
    ss, Bss = alloc("ssB", [16], F32)
    xnb, Bxnb = alloc("xnbB", [1024], BF16)
    scr = (junk, Bjunk, ss, Bss, xnb, Bxnb)
    memt, Bmemt = alloc("memt", [2, D], F32)
    mnT, BmnT = alloc("mnT", [8, 256], BF16)
    kvo, Bkvo = alloc("kvo", [1024], F32)
    KT, BKT = alloc("KT", [8, 256], BF16)
    Vb, BVb = alloc("Vb", [2, D], BF16)
    hnT, BhnT = alloc("hnT", [8, 128], BF16)
    qTP = [alloc("qT%d" % r, [8, 128], BF16) for r in range(2)]
    sc, Bsc = alloc("sc", [4, 256], F32)
    pb, Bpb = alloc("pb", [4, 256], BF16)
    pTP = [alloc("pT%d" % r, [8, 128], BF16) for r in range(2)]
    ob, Bob = alloc("ob", [1024], BF16)
    oT, BoT = alloc("oT", [8, 128], BF16)
    t1x, Bt1x = alloc("t1xB", [1024], F32)
    smxP = [alloc("smx%d" % r, [16], F32) for r in range(2)]
    junk3, Bjunk3 = alloc("junk3", [512], BF16)
    ss3, Bss3 = alloc("ss3", [16], F32)
    def _BA(name, shape, dt_, off):
        return alloc(name, shape, dt_, at=[s_wxk[0] + off, 1 << 20, Buf(name)])
    kfR = [_BA("kf%d" % r, [2, D], F32, r * 8192) for r in range(3)]
    k16R = [_BA("k16%d" % r, [2, D], BF16, 24576 + r * 4096) for r in range(2)]
    kTbR = [alloc("kTb%d" % r, [8, 256], BF16) for r in range(2)]
    qTmR = [alloc("qTm%d" % r, [8, 64], BF16) for r in range(2)]
    pTmR = [alloc("pTm%d" % r, [8, 64], BF16) for r in range(2)]
    for c in range(DC):
        ld("pool", [B_wxk], wxk[:, c, :], w_xk[c * 128:(c + 1) * 128, :])
    for c in range(DC):
        ld("pool", [B_wxv], wxv[:, c, :], w_xv[c * 128:(c + 1) * 128, :])
    for c in range(DC):
        ld("pool", [B_wxq], wxq[:, c, :], w_xq[c * 128:(c + 1) * 128, :])
    for c in range(DC):
        ld("pool", [B_wxo], wxo[:, c, :], w_xo[c * 128:(c + 1) * 128, :])
    ld("sp", [B_gcol], gcol[:, :], colv(mem_nw, 8), slow=True)
    ld("sp", [Bmemt], memt[:, :, :], mem.rearrange("(c p) d -> p c d", p=128))
    for mc in range(2):
        norm_T(memt[:, mc, :], Bmemt, 128, mnT, BmnT, mc * 128, scr)
    for (wt, Bwt, outd, isk) in ((wxk, B_wxk, mk_p, True), (wxv, B_wxv, mv_p, False)):
        for mc in range(2):
            kv_ps = [bankv(0, [512]), bankv(1, [512])]
            for half in range(2):
                for c in range(DC):
                    mm([BmnT, Bwt], [bB[half]], kv_ps[half], mnT[:, c, mc * 128:(mc + 1) * 128], wt[:, c, half * 512:(half + 1) * 512], c == 0, c == DC - 1)
            for half in range(2):
                cp("act", [bB[half]], [Bkvo], kvo[:, half * 512:(half + 1) * 512], kv_ps[half])
            st([Bkvo], outd[mc * 128:(mc + 1) * 128, :], kvo[:, :])
            if not isk:
                cp("dve", [Bkvo], [BVb], Vb[:, mc, :], kvo[:, :])
        if isk:
            for j in range(8):
                kt_ps = bankv(3 + j % 2, [256])
                for c in range(DC):
                    mm([BmnT, Bwt], [bB[3 + j % 2]], kt_ps, wt[:, c, j * 128:(j + 1) * 128], mnT[:, c, :], c == 0, c == DC - 1)
                cp("act", [bB[3 + j % 2]], [BKT], KT[:, j, :], kt_ps)
    ld("sp", [B_gcol], gcol[:, :], colv(ln_xa_pre, 8), slow=True)
    ld("sp", [B_gainb], gainb[:, :], rowb(ln_xa_post, D))

    def b_s1(ti, p):
        T = tile_T(ti)
        qT, BqT = qTP[p]
        norm_T(h_t[:T, ti, :], hB[ti], T, hnT, BhnT, 0, scr, bank=1)
        for half in range(2):
            q_ps = bankv(0, [4, 128])
            for j in range(4):
                for c in range(DC):
                    mm([BhnT, B_wxq], [bB[0]], q_ps[:, j, :T], wxq[:, c, (half * 4 + j) * 128:(half * 4 + j + 1) * 128], hnT[:, c, :T], c == 0, c == DC - 1)
            cp("act", [bB[0]], [BqT], qT[:, half * 4:(half + 1) * 4, :T], q_ps[:, :, :T])

    def b_softmax(T, p):
        smx, Bsmx = smxP[p]
        pT, BpT = pTP[p]
        sc_ps = [bankv(3, [2, 256], parts=T), bankv(4, [2, 256], parts=T)]
        for half in range(2):
            kb.op("dve", lambda e, half=half: e.tensor_reduce(out=smx[:T, half * 2:half * 2 + 2], in_=sc_ps[half], axis=AX.X, op=ALU.max),
                  [bB[3 + half]], [Bsmx], cost=0.7)
        ts("dve", [Bsmx], [Bsmx], out=smx[:T, 4:8], in0=smx[:T, 0:4], scalar1=-1.0 / 16.0, scalar2=None, op0=ALU.mult)
        for hh in range(4):
            act([bB[3 + hh // 2], Bsmx], [Bpb, Bsmx], out=pb[:T, hh, :], in_=sc_ps[hh // 2][:, hh % 2, :], func=AF.Exp, scale=1.0 / 16.0,
                bias=smx[:T, 4 + hh:5 + hh], accum_out=smx[:T, 8 + hh:9 + hh])
        kb.op("dve", lambda e: e.reciprocal(out=smx[:T, 12:16], in_=smx[:T, 8:12]), [Bsmx], [Bsmx])
        pT_ps = bankv(2, [8, 128], BF16)
        for hh in range(4):
            for mc in range(2):
                tr([Bpb, B_identb], [bB[2]], pT_ps[:, hh * 2 + mc, :T], pb[:T, hh, mc * 128:(mc + 1) * 128], identb[:T, :T])
        cp("act", [bB[2]], [BpT], pT[:, :, :T], pT_ps[:, :, :T])

    def b_s2(ti, p):
        T = tile_T(ti)
        qT, BqT = qTP[p]
        sc_ps = [bankv(3, [2, 256], parts=T), bankv(4, [2, 256], parts=T)]
        for hh in range(4):
            for dc in range(2):
                mm([BqT, BKT], [bB[3 + hh // 2]], sc_ps[hh // 2][:, hh % 2, :], qT[:, hh * 2 + dc, :T], KT[:, hh * 2 + dc, :], dc == 0, dc == 1)
        b_softmax(T, p)

    def b_tail(ti, p):
        T = tile_T(ti)
        smx, Bsmx = smxP[p]
        o_ps = [bankv(5, [512], parts=T), bankv(6, [512], parts=T)]
        for half in range(2):
            tt("dve", [bB[5 + half], Bsmx], [Bob], out=ob[:T, half * 512:(half + 1) * 512].rearrange("p (h d) -> p h d", d=256),
               in0=o_ps[half].rearrange("p (h d) -> p h d", d=256), in1=bc(smx[:T, 12 + half * 2:14 + half * 2], [T, 2, 256]), op=ALU.mult)
        oT_ps = bankv(7, [8, 128], BF16)
        for c in range(DC):
            tr([Bob, B_identb], [bB[7]], oT_ps[:, c, :T], ob[:T, c * 128:(c + 1) * 128], identb[:T, :T])
        cp("act", [bB[7]], [BoT], oT[:, :, :T], oT_ps[:, :, :T])
        a_ps = [bankv(5, [512], parts=T), bankv(6, [512], parts=T)]
        for half in range(2):
            for c in range(DC):
                mm([BoT, B_wxo], [bB[5 + half]], a_ps[half], oT[:, c, :T], wxo[:, c, half * 512:(half + 1) * 512], c == 0, c == DC - 1)
        post_norm_residual(a_ps, [bB[5], bB[6]], T, ti, ss3, Bss3, junk3, Bjunk3, t1x, Bt1x)

    def b_s3(ti, p):
        T = tile_T(ti)
        pT, BpT = pTP[p]
        o_ps = [bankv(5, [512], parts=T), bankv(6, [512], parts=T)]
        for hh in range(4):
            for mc in range(2):
                mm([BpT, BVb], [bB[5 + hh // 2]], o_ps[hh // 2][:, (hh % 2) * 256:(hh % 2 + 1) * 256], pT[:, hh * 2 + mc, :T],
                   Vb[:, mc, hh * 256:(hh + 1) * 256], mc == 0, mc == 1)
        b_tail(ti, p)

    def b_sample():
        ti, T, p = NT, 64, 0
        qT, BqT = qTP[p]
        pT, BpT = pTP[p]
        b_s1(ti, p)
        for r in range(2):
            kb.op("pool", lambda e, r=r: e.memset(qTmR[r][0][:, :, :], 0.0), (), [qTmR[r][1]])
            kb.op("pool", lambda e, r=r: e.memset(pTmR[r][0][:, :, :], 0.0), (), [pTmR[r][1]])
        sc_ps = [bankv(3, [2, 256], parts=T), bankv(4, [2, 256], parts=T)]
        o_ps = [bankv(5, [512], parts=T), bankv(6, [512], parts=T)]
        NKR = 3

        def kv_load(src, b):
            kf, Bkf = kfR[b % NKR]
            ld("sp", [Bkf], kf[:, :, :], src[b].rearrange("(c p) d -> p c d", p=128))

        def kv_cast(b):
            kf, Bkf = kfR[b % NKR]
            k16, Bk16 = k16R[b % 2]
            cp("act", [Bkf], [Bk16], k16[:, 0, :], kf[:, 0, :])
            cp("dve", [Bkf], [Bk16], k16[:, 1, :], kf[:, 1, :])
            return k16, Bk16

        for b in range(NKR - 1):
            kv_load(ck, b)
        for b in range(NSQ):
            if b + NKR - 1 < NSQ:
                kv_load(ck, b + NKR - 1)
            k16, Bk16 = kv_cast(b)
            kTb, BkTb = kTbR[b % 2]
            for mc in range(2):
                kt_ps = bankv(mc, [8, 128], BF16)
                for j in range(8):
                    tr([Bk16, B_identb], [bB[mc]], kt_ps[:, j, :], k16[:, mc, j * 128:(j + 1) * 128], identb)
                cp("act" if mc == 0 else "dve", [bB[mc]], [BkTb], kTb[:, :, mc * 128:(mc + 1) * 128], kt_ps[:, :, :])
            qTm, BqTm = qTmR[b % 2]
            if b >= 2:
                kb.op("pool", lambda e, qTm=qTm, pc=(b - 2) * TS: e.memset(qTm[:, :, pc:pc + TS], 0.0), (), [BqTm])
            kb.op("dve", lambda e, qTm=qTm, c0=b * TS: e.tensor_copy(out=qTm[:, :, c0:c0 + TS], in_=qT[:, :, c0:c0 + TS]), [BqT], [BqTm])
            for hh in range(4):
                for dc in range(2):
                    mm([BqTm, BkTb], [bB[3 + hh // 2]], sc_ps[hh // 2][:, hh % 2, :], qTm[:, hh * 2 + dc, :], kTb[:, hh * 2 + dc, :],
                       b == 0 and dc == 0 and hh % 2 == 0, b == NSQ - 1 and dc == 1, skip=True)
        for b in range(NKR - 1):
            kv_load(cv, b)
        b_softmax(T, p)
        for b in range(NSQ):
            if b + NKR - 1 < NSQ:
                kv_load(cv, b + NKR - 1)
            k16, Bk16 = kv_cast(b)
            pTm, BpTm = pTmR[b % 2]
            if b >= 2:
                kb.op("pool", lambda e, pTm=pTm, pc=(b - 2) * TS: e.memset(pTm[:, :, pc:pc + TS], 0.0), (), [BpTm])
            kb.op("dve", lambda e, pTm=pTm, c0=b * TS: e.tensor_copy(out=pTm[:, :, c0:c0 + TS], in_=pT[:, :, c0:c0 + TS]), [BpT], [BpTm])
            for hh in range(4):
                for mc in range(2):
                    mm([BpTm, Bk16], [bB[5 + hh // 2]], o_ps[hh // 2][:, (hh % 2) * 256:(hh % 2 + 1) * 256], pTm[:, hh * 2 + mc, :],
                       k16[:, mc, hh * 256:(hh + 1) * 256], b == 0 and mc == 0 and hh % 2 == 0, b == NSQ - 1 and mc == 1, skip=True)
        b_tail(ti, p)

    def phaseB_all(tiles):
        ptiles = [t for t in tiles if t < NT]
        n = len(ptiles)
        for i in range(n + 2):
            streams = []
            leads = []
            if i < n:
                streams.append(kb.capture(lambda: b_s1(ptiles[i], i % 2)))
                leads.append(0.0)
            if 0 <= i - 1 < n:
                streams.append(kb.capture(lambda: b_s2(ptiles[i - 1], (i - 1) % 2)))
                leads.append(0.05)
            if 0 <= i - 2 < n:
                streams.append(kb.capture(lambda: b_s3(ptiles[i - 2], (i - 2) % 2)))
                leads.append(0.10)
            kb.replay(interleave_prop(*streams, lead=leads))
        if NT in tiles:
            kb.barrier()
            b_sample()

    tilesB = list(range(NT + 1))
    if SAMPLE_SSD_DISABLED:
        tilesB = list(range(NT))
    if stop is not None and stop[0] == "B":
        tilesB = stop[1]
    if stop is not None and stop[0] == "X":
        tilesB = stop[4]
    phaseB_all(tilesB)
    if stop is not None and stop[0] == "B":
        kb.emit()
        return nc

    if debug:
        dbg_h2 = dout("dbg_h2", [SEQ + 64, D])
        for ti in range(NT + 1):
            T = tile_T(ti)
            st([hB[ti]], dbg_h2[ti * 128:ti * 128 + T, :], h_t[:T, ti, :])

    kb.barrier()
    alloc.reset()
    wg, B_wg = alloc("wg", [8, DFF], BF16)
    wd, B_wd = alloc("wd", [FC, D], BF16)
    NRING = 2
    ring = [alloc("wu%d" % r, [8, 256], BF16) for r in range(NRING)]
    s_jx = alloc.slot("s_jx", 2048)
    junk, Bjunk = alloc("junkC", [1024], BF16, at=s_jx)
    ss, Bss = alloc("ssC", [16], F32)
    ss3, Bss3 = alloc("ssC3", [16], F32)
    xnb, Bxnb = alloc("xnbC", [1024], BF16, at=s_jx)
    scr = (junk, Bjunk, ss, Bss, xnb, Bxnb)
    hfT, BhfT = alloc("hfT", [8, 512], BF16)
    actT, BactT = alloc("actT", [FC, 512], BF16)
    sgt = [alloc("sgt%d" % r, [512], F32) for r in range(2)]
    junk3 = sgt[0][0].bitcast(BF16)
    Bjunk3 = sgt[0][1]
    tmp3, Btmp3 = sgt[1]

    for c in range(DC):
        ld("pool", [B_wg], wg[:, c, :], w_gate[c * 128:(c + 1) * 128, :])
    ld("sp", [B_gcol], gcol[:, :], colv(ln_ffn_pre, 8), slow=True)
    ld("sp", [B_gainb], gainb[:, :], rowb(ln_ffn_post, D))

    groups = [(list(range(g * 4, g * 4 + 4))) for g in range(4)] + [[NT]]
    gcols = {}

    def c_norms(gi):
        col = 0
        cols = []
        for ti in groups[gi]:
            T = tile_T(ti)
            norm_T(h_t[:T, ti, :], hB[ti], T, hfT, BhfT, col, scr)
            cols.append(col)
            col += T
        gcols[gi] = cols

    def c_down(gi):
        cols = gcols[gi]
        for k, ti in enumerate(groups[gi]):
            T = tile_T(ti)
            f_ps = [bankv(5, [512], parts=T), bankv(6, [512], parts=T)]
            for half in range(2):
                for f in range(FC):
                    mm([BactT, B_wd], [bB[5 + half]], f_ps[half], actT[:, f, cols[k]:cols[k] + T], wd[:, f, half * 512:(half + 1) * 512], f == 0, f == FC - 1)
            post_norm_residual(f_ps, [bB[5], bB[6]], T, ti, ss3, Bss3, junk3, Bjunk3, tmp3, Btmp3)
            if ti < NT:
                st([hB[ti]], y_p[ti * 128:(ti + 1) * 128, :], h_t[:, ti, :])
            else:
                st([hB[ti]], y_s[:, :], h_t[:64, ti, :])

    c_norms(0)
    ring_i = 0
    for gi, grp in enumerate(groups):
        GT = sum(tile_T(ti) for ti in grp)
        for fp in range(FC // 2):
            wu, B_wu = ring[ring_i % NRING]
            ring_i += 1
            ld("pool", [B_wu], wu[:, :, :], w_up[:, fp * 256:(fp + 1) * 256].rearrange("(c p) n -> p c n", p=128))
            if gi == 0 and fp == 1:
                for f in range(FC):
                    ld("pool", [B_wd], wd[:, f, :], w_down[f * 128:(f + 1) * 128, :])
            for sub_ in range(2):
                f = fp * 2 + sub_
                pi = f % 2
                g_ps = bankv(pi * 2, [512])
                u_ps = bankv(pi * 2 + 1, [512])
                for c in range(DC):
                    mm([BhfT, B_wg], [bB[pi * 2]], g_ps[:, :GT], wg[:, c, f * 128:(f + 1) * 128], hfT[:, c, :GT], c == 0, c == DC - 1)
                for c in range(DC):
                    mm([BhfT, B_wu], [bB[pi * 2 + 1]], u_ps[:, :GT], wu[:, c, sub_ * 128:(sub_ + 1) * 128], hfT[:, c, :GT], c == 0, c == DC - 1)
                sg_, Bsg_ = sgt[pi]
                act([bB[pi * 2]], [Bsg_], out=sg_[:, :GT], in_=g_ps[:, :GT], func=AF.Silu)
                tt("dve", [bB[pi * 2 + 1], Bsg_], [BactT], out=actT[:, f, :GT], in0=u_ps[:, :GT], in1=sg_[:, :GT], op=ALU.mult)
        sA = kb.capture(lambda: c_down(gi))
        sB = kb.capture(lambda: c_norms(gi + 1)) if gi + 1 < len(groups) else []
        kb.replay(interleave_prop(sA, sB))

    kb.emit()
    return nc


_CACHE = {}


def _consts():
    c = np.zeros((128, NCONST), np.float32)
    i = np.arange(128)
    c[:, C_ID:C_ID + 128] = np.eye(128, dtype=np.float32)
    c[:, C_MT:C_MT + 128] = (i[:, None] <= i[None, :]).astype(np.float32)
    c[:, C_LS:C_LS + 128] = (i[None, :] < i[:, None]).astype(np.float32)
    c[:, C_ONE:C_ONE + 128] = 1.0
    j = np.arange(64)
    sameseq = (j[:, None] // TS) == (j[None, :] // TS)
    c[:64, C_MTS:C_MTS + 64] = (sameseq & (j[:, None] <= j[None, :])).astype(np.float32)
    c[:64, C_LSS:C_LSS + 64] = (sameseq & (j[None, :] < j[:, None])).astype(np.float32)
    c[:64, C_SAME:C_SAME + 64] = sameseq.astype(np.float32)
    c[:64, C_SIND:C_SIND + 16] = ((j[:, None] // TS) == np.arange(16)[None, :]).astype(np.float32)
    c[:, C_NH] = -0.5
    c[:64, C_HM] = 1.0
    c[64:, C_HM + 1] = 1.0
    mrow = ((j[None, :] // TS) == np.arange(16)[:, None]).astype(np.float32).reshape(1, 1024)
    c[:, C_MROW:C_MROW + 1024] = mrow
    return c


def kernel(**inputs):
    debug = bool(inputs.pop("_debug", False))
    stop = inputs.pop("_stop", None)
    key = ("nc", debug, str(stop))
    if key not in _CACHE:
        _CACHE[key] = build_program(debug, stop)
    nc = _CACHE[key]
    f = lambda a: np.ascontiguousarray(np.asarray(a, dtype=np.float32))
    consts = _consts()
    shared = {}
    for name in ("ln_mix_pre", "ln_mix_post", "w_in", "gla_gate_w2", "gla_gate_b", "gla_norm_w", "ssd_conv_w", "ssd_conv_b",
                 "ssd_dt_bias", "ssd_A_log", "ssd_D", "ssd_norm_w", "w_out", "ln_xa_pre", "ln_xa_post", "mem_norm_w",
                 "w_xq", "w_xk", "w_xv", "w_xo", "ln_ffn_pre", "ln_ffn_post", "w_gate", "w_up", "w_down"):
        shared[name] = f(inputs[name])[0]
    xp = f(inputs["x_prompt"]); xs = f(inputs["x_sample"]); memp = f(inputs["mem_prompt"])
    sg = f(inputs["state_gla"])[0]; ssm = f(inputs["state_ssm"])[0]; scv = f(inputs["state_conv"])[0]
    ckk = f(inputs["cache_mem_k"])[0]; cvv = f(inputs["cache_mem_v"])[0]
    in_maps = []
    for c in range(8):
        m = dict(shared)
        sl = slice(c * NSQ, (c + 1) * NSQ)
        m["x_p"] = xp[c]
        m["x_s"] = np.ascontiguousarray(xs[sl].reshape(64, D))
        m["mem"] = memp[c]
        m["sgla"] = np.ascontiguousarray(sg[sl])
        m["sssm"] = np.ascontiguousarray(ssm[sl])
        m["sconv"] = np.ascontiguousarray(scv[sl].reshape(NSQ * 3, D))
        m["ck"] = np.ascontiguousarray(ckk[sl].reshape(NSQ, 256, D))
        m["cv"] = np.ascontiguousarray(cvv[sl].reshape(NSQ, 256, D))
        m["consts"] = consts
        in_maps.append(m)
    res = run_bass_kernel_spmd(nc, in_maps, core_ids=list(range(8)))
    rs = res.results
    cat = lambda k: np.stack([np.asarray(r[k], dtype=np.float32) for r in rs], axis=0)
    y_prompt = cat("y_p")
    y_sample = cat("y_s").reshape(128, TS, D)
    gla_prompt = cat("gla_p")[None]
    ssm_prompt = cat("ssm_p").reshape(8, 8, 64, 128)[None]
    conv_prompt = cat("conv_p")[None]
    mk_prompt = cat("mk_p").reshape(8, 256, 4, 256)[None]
    mv_prompt = cat("mv_p").reshape(8, 256, 4, 256)[None]
    gla_sample = cat("gla_s").reshape(128, 4, 64, 128)[None]
    ssm_sample = cat("ssm_s").reshape(128, 8, 64, 128)[None]
    conv_sample = cat("conv_s").reshape(128, 3, D)[None]
    out = (y_prompt, y_sample, gla_prompt, ssm_prompt, conv_prompt, mk_prompt, mv_prompt, gla_sample, ssm_sample, conv_sample)
    if debug:
        return out, {k: cat(k) for k in ("dbg_h1", "dbg_h2")}
    return out
```
